# Optimizing a Trainium2 kernel written in Bass

```python
import math
import jax, jax.numpy as jnp
from jax import lax
import numpy as np

D_MODEL = 1024
BATCH = 8
SEQ = 2048
DEPTH = 2
DEC_BATCH = 128
DEC_SEQ = 1
PAST_LEN = 16384
PAGE_SIZE = 128

D_MIX = D_MODEL
N_MIXERS = 4
GROUP_W = D_MIX // N_MIXERS
HEADS_PER_MIXER = 4
HEAD_DIM = GROUP_W // HEADS_PER_MIXER
IN_WIDTH = 8 * GROUP_W
CONV_A_WIDTH = 31
CONV_D_WIDTH = 3
POOL_WINDOWS = (2, 4, 8, 16)
POOL_BUF = max(POOL_WINDOWS) - 1
CHUNK = 128
D_FF = -(-8 * D_MODEL // (3 * 256)) * 256
N_MOD = 6
RMS_EPS = 1e-6
LN_EPS = 1e-5

kernel_name = 'hymba_conv_pool_gmlp_shortconv_decoder'


def rmsnorm(x, g):
    xf = x.astype(jnp.float32)
    y = xf * lax.rsqrt(jnp.mean(xf * xf, axis=-1, keepdims=True) + RMS_EPS)
    return (y * g.astype(jnp.float32)).astype(x.dtype)


def layernorm(x, g, b):
    xf = x.astype(jnp.float32)
    mu = jnp.mean(xf, axis=-1, keepdims=True)
    var = jnp.mean(jnp.square(xf - mu), axis=-1, keepdims=True)
    y = (xf - mu) * lax.rsqrt(var + LN_EPS)
    return (y * g.astype(jnp.float32) + b.astype(jnp.float32)).astype(x.dtype)


def causal_depthwise(z_ext, w):
    c = z_ext.shape[-1]
    return lax.conv_general_dilated(z_ext, w[:, None, :].astype(z_ext.dtype), (1,), 'VALID',
                                    dimension_numbers=('NWC', 'WIO', 'NWC'),
                                    feature_group_count=c)


def multiscale_pool(z_ext, n_prev, w_lin, scale):
    bsz, length, c = z_ext.shape
    t_new = length - n_prev
    gc = c // len(POOL_WINDOWS)
    zf = z_ext.astype(jnp.float32)
    csum = jnp.concatenate([jnp.zeros((bsz, 1, c), jnp.float32), jnp.cumsum(zf, axis=1)], axis=1)
    j = np.arange(n_prev, length)
    outs = []
    for g, w in enumerate(POOL_WINDOWS):
        s_g = csum[..., g * gc:(g + 1) * gc]
        start = np.maximum(j + 1 - w, 0)
        cnt = jnp.asarray((j + 1 - start).astype(np.float32))
        mean = (jnp.take(s_g, j + 1, axis=1) - jnp.take(s_g, start, axis=1)) / cnt[None, :, None]
        outs.append(mean - zf[:, n_prev:, g * gc:(g + 1) * gc])
    d = jnp.stack(outs, axis=2).astype(z_ext.dtype)
    y = jnp.einsum('btgc,gcd->btgd', d, w_lin).reshape(bsz, t_new, c)
    return y * scale


def spatial_gating(v, w_s, b_s):
    bsz, t, c = v.shape
    nc = -(-t // CHUNK)
    vp = jnp.pad(v, ((0, 0), (0, nc * CHUNK - t), (0, 0)))
    vp = vp.reshape(bsz, nc, CHUNK, HEADS_PER_MIXER, c // HEADS_PER_MIXER)
    mask = np.tril(np.ones((CHUNK, CHUNK), dtype=bool))
    wm = jnp.where(mask[None], w_s, jnp.zeros_like(w_s))
    out = jnp.einsum('hij,bnjhc->bnihc', wm, vp) + b_s.T[None, None, :, :, None]
    return out.reshape(bsz, nc * CHUNK, c)[:, :t]


def token_mixer(h, buf_a, buf_b, buf_d, p):
    proj = h @ p['w_in']
    a_val, a_gate, b_in, c_u, c_v, d_b, d_c, d_h = jnp.split(proj, 8, axis=-1)
    za = a_val * jax.nn.sigmoid(a_gate)
    za_ext = jnp.concatenate([buf_a, za], axis=1)
    ya = causal_depthwise(za_ext, p['conv_a_w']) + p['conv_a_b']
    ya = jax.nn.silu(layernorm(ya, p['ln_a_g'], p['ln_a_b']))
    zb_ext = jnp.concatenate([buf_b, b_in], axis=1)
    yb = multiscale_pool(zb_ext, buf_b.shape[1], p['pool_w'], p['pool_scale'])
    vn = layernorm(c_v, p['ln_c_g'], p['ln_c_b'])
    yc = c_u * spatial_gating(vn, p['sgu_w'], p['sgu_b'])
    zd_ext = jnp.concatenate([buf_d, d_c * d_h], axis=1)
    yd = d_b * causal_depthwise(zd_ext, p['conv_d_w'])
    out = jnp.concatenate([ya, yb, yc, yd], axis=-1) @ p['w_out']
    return (out, za_ext[:, -(CONV_A_WIDTH - 1):], zb_ext[:, -POOL_BUF:],
            zd_ext[:, -(CONV_D_WIDTH - 1):], vn)


def decoder_layer(x, c, buf_a, buf_b, buf_d, p):
    mod = (jax.nn.silu(c) @ p['w_ada'] + p['b_ada'])[:, None, :]
    sh1, sc1, gt1, sh2, sc2, gt2 = jnp.split(mod, N_MOD, axis=-1)
    h = rmsnorm(x, p['g_pre_mix']) * (1 + sc1) + sh1
    m, na, nb, nd, vn = token_mixer(h, buf_a, buf_b, buf_d, p)
    x = x + gt1 * rmsnorm(m, p['g_post_mix'])
    h = rmsnorm(x, p['g_pre_ffn']) * (1 + sc2) + sh2
    f = (jax.nn.silu(h @ p['w_gate']) * (h @ p['w_up'])) @ p['w_down']
    x = x + gt2 * rmsnorm(f, p['g_post_ffn'])
    return x, na, nb, nd, vn


def setup_inputs(seed: int = 0) -> dict:
    key = jax.random.key(seed)
    ks = jax.random.split(key, 32)
    nrm = jax.random.normal
    f32 = jnp.float32
    L, D, G = DEPTH, D_MODEL, GROUP_W
    def gain(k, shape):
        return 1.0 + 0.05 * nrm(k, shape, f32)
    return {
        'x_prompt': nrm(ks[0], (BATCH, SEQ, D), f32),
        'x_sample': nrm(ks[1], (DEC_BATCH, DEC_SEQ, D), f32),
        'c_prompt': nrm(ks[2], (BATCH, D), f32),
        'c_sample': nrm(ks[3], (DEC_BATCH, D), f32),
        'state_conv_a': 0.5 * nrm(ks[4], (L, DEC_BATCH, CONV_A_WIDTH - 1, G), f32),
        'state_pool_b': nrm(ks[5], (L, DEC_BATCH, POOL_BUF, G), f32),
        'state_conv_d': 0.5 * nrm(ks[6], (L, DEC_BATCH, CONV_D_WIDTH - 1, G), f32),
        'w_ada': nrm(ks[7], (L, D, N_MOD * D), f32) * D ** -0.5,
        'b_ada': 0.01 * nrm(ks[8], (L, N_MOD * D), f32),
        'g_pre_mix': gain(ks[9], (L, D)),
        'g_post_mix': gain(ks[10], (L, D)),
        'w_in': nrm(ks[11], (L, D, IN_WIDTH), f32) * D ** -0.5,
        'conv_a_w': nrm(ks[12], (L, CONV_A_WIDTH, G), f32) * CONV_A_WIDTH ** -0.5,
        'conv_a_b': 0.02 * nrm(ks[13], (L, G), f32),
        'ln_a_g': gain(ks[14], (L, G)),
        'ln_a_b': 0.02 * nrm(ks[15], (L, G), f32),
        'pool_w': nrm(ks[16], (L, len(POOL_WINDOWS), G // len(POOL_WINDOWS), G // len(POOL_WINDOWS)), f32) * (G // len(POOL_WINDOWS)) ** -0.5,
        'pool_scale': gain(ks[17], (L, G)),
        'ln_c_g': gain(ks[18], (L, G)),
        'ln_c_b': 0.02 * nrm(ks[19], (L, G), f32),
        'sgu_w': nrm(ks[20], (L, HEADS_PER_MIXER, CHUNK, CHUNK), f32) * CHUNK ** -0.5,
        'sgu_b': 1.0 + 0.1 * nrm(ks[21], (L, HEADS_PER_MIXER, CHUNK), f32),
        'conv_d_w': nrm(ks[22], (L, CONV_D_WIDTH, G), f32) * CONV_D_WIDTH ** -0.5,
        'w_out': nrm(ks[23], (L, D_MIX, D), f32) * D_MIX ** -0.5,
        'g_pre_ffn': gain(ks[24], (L, D)),
        'g_post_ffn': gain(ks[25], (L, D)),
        'w_gate': nrm(ks[26], (L, D, D_FF), f32) * D ** -0.5,
        'w_up': nrm(ks[27], (L, D, D_FF), f32) * D ** -0.5,
        'w_down': nrm(ks[28], (L, D_FF, D), f32) * D_FF ** -0.5,
    }


def reference(x_prompt, x_sample, c_prompt, c_sample, state_conv_a, state_pool_b, state_conv_d,
              w_ada, b_ada, g_pre_mix, g_post_mix, w_in, conv_a_w, conv_a_b, ln_a_g, ln_a_b,
              pool_w, pool_scale, ln_c_g, ln_c_b, sgu_w, sgu_b, conv_d_w, w_out,
              g_pre_ffn, g_post_ffn, w_gate, w_up, w_down):
    bp, dt = x_prompt.shape[0], x_prompt.dtype
    buf_a_p = jnp.zeros((bp, CONV_A_WIDTH - 1, GROUP_W), dt)
    buf_b_p = jnp.zeros((bp, 0, GROUP_W), dt)
    buf_d_p = jnp.zeros((bp, CONV_D_WIDTH - 1, GROUP_W), dt)
    xp, xs = x_prompt, x_sample
    na_p, nb_p, nd_p, na_s, nb_s, nd_s, nv_s = [], [], [], [], [], [], []
    for l in range(DEPTH):
        p = {'w_ada': w_ada[l], 'b_ada': b_ada[l], 'g_pre_mix': g_pre_mix[l], 'g_post_mix': g_post_mix[l],
             'w_in': w_in[l], 'conv_a_w': conv_a_w[l], 'conv_a_b': conv_a_b[l], 'ln_a_g': ln_a_g[l],
             'ln_a_b': ln_a_b[l], 'pool_w': pool_w[l], 'pool_scale': pool_scale[l], 'ln_c_g': ln_c_g[l],
             'ln_c_b': ln_c_b[l], 'sgu_w': sgu_w[l], 'sgu_b': sgu_b[l], 'conv_d_w': conv_d_w[l],
             'w_out': w_out[l], 'g_pre_ffn': g_pre_ffn[l], 'g_post_ffn': g_post_ffn[l],
             'w_gate': w_gate[l], 'w_up': w_up[l], 'w_down': w_down[l]}
        xp, a_p, b_p, d_p, _ = decoder_layer(xp, c_prompt, buf_a_p, buf_b_p, buf_d_p, p)
        xs, a_s, b_s, d_s, v_s = decoder_layer(xs, c_sample, state_conv_a[l], state_pool_b[l],
                                               state_conv_d[l], p)
        na_p.append(a_p); nb_p.append(b_p); nd_p.append(d_p)
        na_s.append(a_s); nb_s.append(b_s); nd_s.append(d_s); nv_s.append(v_s)
    return (xp, xs,
            jnp.stack(na_p), jnp.stack(nb_p), jnp.stack(nd_p),
            jnp.stack(na_s), jnp.stack(nb_s), jnp.stack(nd_s), jnp.stack(nv_s))
```

```python
import contextlib
import numpy as np
import concourse.bass as bass
import concourse.mybir as mybir
from concourse.bass_utils import run_bass_kernel_spmd

F32 = mybir.dt.float32
F32R = mybir.dt.float32r
AF = mybir.ActivationFunctionType
ALU = mybir.AluOpType
AX = mybir.AxisListType

D = 1024
SEQ = 2048
NT = 512
NTILES = SEQ // NT
NS = 16
DFF = 2816
NFF = DFF // 128
RMS_EPS = 1e-6
LN_EPS = 1e-5
POOLW = (2, 4, 8, 16)

C_BADA, C_GPM, C_GOM, C_GPF, C_GOF, C_CAB, C_LAG, C_LAB, C_PSC, C_CDW = 0, 48, 56, 64, 72, 80, 82, 84, 86, 88
NS1 = 94
C_CAW = 94
NPAR = 94 + 62


class Reg:
    __slots__ = ("w", "rs", "name", "psum")

    def __init__(self, name="", psum=False):
        self.w = None
        self.rs = {}
        self.name = name
        self.psum = psum


class Eng:
    def __init__(self, name, handle, sem):
        self.name = name
        self.h = handle
        self.sem = sem
        self.cnt = 0
        self.waited = {}


class DSem:
    def __init__(self, sem):
        self.sem = sem
        self.cnt = 0


class Prog:
    def __init__(self, nc, es):
        self.nc = nc
        self.es = es
        self.n_sem = 0
        self.pe = self._eng("pe", nc.tensor)
        self.dve = self._eng("dve", nc.vector)
        self.act = self._eng("act", nc.scalar)
        self.pool = self._eng("pool", nc.gpsimd)
        self.sp = self._eng("sp", nc.sync)
        self.engs = [self.pe, self.dve, self.act, self.pool, self.sp]
        self.out_tokens = []
        self.n_inst = 0

    def new_sem(self, name):
        self.n_sem += 1
        return self.es.enter_context(self.nc.semaphore(name))

    def _eng(self, name, h):
        return Eng(name, h, self.new_sem("prog_" + name))

    def dsem(self, name):
        return DSem(self.new_sem("d_" + name))

    def sb(self, name, shape, dtype=F32):
        return self.es.enter_context(self.nc.sbuf_tensor(name, list(shape), dtype))

    def ps(self, name, shape, dtype=F32):
        return self.es.enter_context(self.nc.psum_tensor(name, list(shape), dtype))

    def _wait(self, eng, reads, writes):
        need = {}

        def add(tok, raw):
            if tok is None:
                return
            sem, val, src = tok
            if src is eng and not raw and eng.name == "pe":
                return
            k = id(sem)
            if k not in need or need[k][1] < val:
                need[k] = (sem, val)

        for r in reads:
            add(r.w, True)
            if r.psum:
                for tok in r.rs.values():
                    if tok[2] is not eng:
                        add(tok, False)
        for w in writes:
            add(w.w, False)
            for tok in w.rs.values():
                add(tok, False)
        for k, (sem, val) in need.items():
            if eng.waited.get(k, 0) < val:
                eng.h.wait_ge(sem, val)
                eng.waited[k] = val

    def _commit(self, tok, reads, writes):
        k = id(tok[0])
        for r in reads:
            old = r.rs.get(k)
            if old is None or old[1] < tok[1]:
                r.rs[k] = tok
        for w in writes:
            w.w = tok
            w.rs = {}

    def op(self, eng, fn, reads=(), writes=(), inc=True):
        self._wait(eng, reads, writes)
        inst = fn()
        self.n_inst += 1
        if inc:
            eng.cnt += 1
            inst.then_inc(eng.sem, 1)
            tok = (eng.sem, eng.cnt, eng)
        else:
            tok = (eng.sem, eng.cnt + 1, eng)
        self._commit(tok, reads, writes)
        return tok

    def dma(self, q, ds, pairs, reads=(), writes=(), is_output=False):
        self._wait(q, reads, writes)
        for (o, i) in pairs:
            q.h.dma_start(out=o, in_=i).then_inc(ds.sem, 16)
            ds.cnt += 16
            self.n_inst += 1
        tok = (ds.sem, ds.cnt, None)
        self._commit(tok, reads, writes)
        if is_output:
            self.out_tokens.append(tok)
        return tok

    def finish(self):
        need = {}
        for (sem, val, _) in self.out_tokens:
            k = id(sem)
            if k not in need or need[k][1] < val:
                need[k] = (sem, val)
        for e in self.engs:
            if e is not self.sp and e.cnt > 0:
                need[id(e.sem)] = (e.sem, e.cnt)
        for k, (sem, val) in need.items():
            self.sp.h.wait_ge(sem, val)


def build_program():
    nc = bass.Bass("TRN2", target_bir_lowering=False)

    def din(name, shape):
        return nc.dram_tensor(name, list(shape), F32, kind="ExternalInput").ap()

    def dout(name, shape):
        return nc.dram_tensor(name, list(shape), F32, kind="ExternalOutput").ap()

    xp = din("xp", [SEQ, D]); xs = din("xs", [NS, D]); cc = din("cc", [NS + 1, D])
    sa = din("sa", [2, NS, 30, 256]); sbb = din("sbb", [2, NS, 15, 256]); sd = din("sd", [2, NS, 2, 256])
    w_ada = din("w_ada", [2, D, 6 * D]); b_ada = din("b_ada", [2, 6 * D])
    g_pre_mix = din("g_pre_mix", [2, D]); g_post_mix = din("g_post_mix", [2, D])
    w_in = din("w_in", [2, D, 2048]); conv_a_w = din("conv_a_w", [2, 31, 256]); conv_a_b = din("conv_a_b", [2, 256])
    ln_a_g = din("ln_a_g", [2, 256]); ln_a_b = din("ln_a_b", [2, 256])
    pool_w = din("pool_w", [2, 4, 64, 64]); pool_scale = din("pool_scale", [2, 256])
    ln_c_g = din("ln_c_g", [2, 256]); ln_c_b = din("ln_c_b", [2, 256])
    sgu_w = din("sgu_w", [2, 4, 128, 128]); sgu_b = din("sgu_b", [2, 4, 128])
    conv_d_w = din("conv_d_w", [2, 3, 256]); w_out = din("w_out", [2, D, D])
    g_pre_ffn = din("g_pre_ffn", [2, D]); g_post_ffn = din("g_post_ffn", [2, D])
    w_gate = din("w_gate", [2, D, DFF]); w_up = din("w_up", [2, D, DFF]); w_down = din("w_down", [2, DFF, D])
    c_ident = din("c_ident", [128, 128]); c_tril = din("c_tril", [128, 128])
    c_invw = din("c_invw", [128, 2]); c_invcnt = din("c_invcnt", [128, 2, 16])

    yp = dout("yp", [SEQ, D]); ys = dout("ys", [NS, D])
    nap = dout("nap", [2, 30, 256]); nbp = dout("nbp", [2, 15, 256]); ndp = dout("ndp", [2, 2, 256])
    nas = dout("nas", [2, NS, 30, 256]); nbs = dout("nbs", [2, NS, 15, 256]); nds = dout("nds", [2, NS, 2, 256])
    nvs = dout("nvs", [2, NS, 256])

    import os as _os
    DBG = _os.environ.get("KDBG") == "1"
    if DBG:
        dbg_yc = dout("dbg_yc", [128, 8, NT]); dbg_x1 = dout("dbg_x1", [128, 8, NT]); dbg_m = dout("dbg_m", [128, 8, NT])
        dbg_x2 = dout("dbg_x2", [128, 8, NT]); dbg_h2 = dout("dbg_h2", [128, 8, NT]); dbg_f = dout("dbg_f", [128, 8, NT])
        dbg_hact = dout("dbg_hact", [128, NFF, NT])
    es = contextlib.ExitStack()
    with es:
        P = Prog(nc, es)
        pe, dve, act, pool, sp = P.pe, P.dve, P.act, P.pool, P.sp
        V, S, G, T = nc.vector, nc.scalar, nc.gpsimd, nc.tensor

        xT = P.sb("xT", [128, 8, NT]); xT_r = [Reg("xT%d" % i) for i in range(8)]
        hT = P.sb("hT", [128, 8, NT]); hT_r = [Reg("hT%d" % i) for i in range(8)]
        yc = P.sb("yc", [128, 8, NT]); yc_r = [Reg("yc%d" % i) for i in range(8)]
        big = P.sb("big", [128, NFF, NT]); big_r = [Reg("big%d" % i) for i in range(NFF)]
        NSLOT = 3
        wsl = [P.sb("wsl%d" % i, [128, 4096]) for i in range(NSLOT)]
        wsl_r = [Reg("wsl%d" % i) for i in range(NSLOT)]
        wsl_d = [P.dsem("wsl%d" % i) for i in range(NSLOT)]
        pb = [P.ps("pb%d" % i, [128, 512]) for i in range(8)]
        pb_r = [Reg("pb%d" % i, psum=True) for i in range(8)]
        pbi = [0]

        held = set()

        def nb(hold=False):
            while True:
                k = pbi[0] % 8
                pbi[0] += 1
                if k not in held:
                    break
            if hold:
                held.add(k)
            return pb[k], pb_r[k]

        def tile(name, shape):
            return P.sb(name, shape), Reg(name)

        ident, ident_r = tile("ident", [128, 128])
        tril, tril_r = tile("tril", [128, 128])
        ones, ones_r = tile("ones", [128, 128])
        invw, invw_r = tile("invw", [128, 2])
        invcnt, invcnt_r = tile("invcnt", [128, 2, 16])
        stg = [tile("stg%d" % i, [128, 1024]) for i in range(2)]
        stg_d = [P.dsem("stg%d" % i) for i in range(2)]
        ost = stg
        ost_d = stg_d
        parT = [tile("parT%d" % l, [128, NPAR]) for l in range(2)]
        gcb = [tile("gcb%d" % l, [128, 2, 256]) for l in range(2)]
        bsb = [tile("bsb%d" % l, [128, 2, 128]) for l in range(2)]
        w00 = [tile("w00%d" % l, [128, 2]) for l in range(2)]
        wmT = [tile("wmT%d" % l, [128, 4, 128]) for l in range(2)]
        BD = [tile("BD%d" % l, [128, 2, 128]) for l in range(2)]
        modT = [tile("modT%d" % l, [128, 48, NS + 1]) for l in range(2)]
        A1 = [tile("A1_%d" % l, [128, 8, NS + 1]) for l in range(2)]
        G1 = [tile("G1_%d" % l, [128, 8, NS + 1]) for l in range(2)]
        A2 = [tile("A2_%d" % l, [128, 8, NS + 1]) for l in range(2)]
        G2 = [tile("G2_%d" % l, [128, 8, NS + 1]) for l in range(2)]
        scT, scT_r = tile("scT", [128, 8, NS + 1])
        haloA = [tile("haloA%d" % l, [128, 2, 30]) for l in range(2)]
        haloB = [tile("haloB%d" % l, [128, 2, 15]) for l in range(2)]
        haloD = [tile("haloD%d" % l, [128, 2, 2]) for l in range(2)]
        zA = big[:, 16:19, :].rearrange("p a b -> p (a b)")[:, 0:2 * (30 + NT)].rearrange("p (c n) -> p c n", c=2)
        zA_r = big_r[16:19]
        zB = big[:, 19:22, :].rearrange("p a b -> p (a b)")[:, 0:2 * (15 + NT)].rearrange("p (c n) -> p c n", c=2)
        zB_r = big_r[19:22]
        zD, zD_r0 = tile("zD", [128, 2, 2 + NT])
        zD_r = [zD_r0]
        acc = [tile("acc%d" % i, [128, 15 + NT]) for i in range(4)]
        sq = [tile("sq%d" % i, [128, 15 + NT]) for i in range(2)]
        rstd, rstd_r = tile("rstd", [128, NT])
        ssum, ssum_r = tile("ssum", [128, NT])
        tm = [tile("tm%d" % i, [128, 15 + NT]) for i in range(3)]
        PQ = [tm[0], sq[0], tm[1], sq[1]]
        cv = big[:, 8:10, :].rearrange("p a (t c) -> p (a t) c", c=256)
        cv_r = big_r[8:10]
        bst, bst_r = tile("bst", [128, 4, 6])
        bmv, bmv_r = tile("bmv", [128, 4, 2])
        brs, brs_r = tile("brs", [128, 4])
        stAT, stAT_r = tile("stAT", [128, 2, NS * 30])
        stBT, stBT_r = tile("stBT", [128, 2, NS * 15])
        stDT, stDT_r = tile("stDT", [128, 2, NS * 2])
        prd, prd_r = tm[2][0][:, 0:NS * 30], tm[2][1]
        red, red_r = tile("red", [128, 2, NS])
        sm = stg
        sm_d = stg_d
        smi = [0]
        d_misc = P.dsem("misc")
        d_c2c = P.dsem("c2c")

        def wview8(i, cols):
            return wsl[i][:, 0:8 * cols].rearrange("p (k c) -> p k c", k=8)

        def wview22(i):
            return wsl[i][:, 0:NFF * 128].rearrange("p (k c) -> p k c", k=NFF)

        def wraw(i):
            return wsl[i][:, 0:2048].rearrange("p (k c) -> p k c", k=8)

        def _addr(t):
            return nc.lookup_mloc(t.name).addr

        wlo_t = [nc.alloc_sbuf_tensor_at("wlo%d" % i, [128, 2048], F32R, offset=_addr(wsl[i])) for i in range(NSLOT)]
        whi_t = [nc.alloc_sbuf_tensor_at("whi%d" % i, [128, 2048], F32R, offset=_addr(wsl[i]) + 2048 * 4) for i in range(NSLOT)]
        hhi_t = nc.alloc_sbuf_tensor_at("hhi", [128, 8, NT], F32R, offset=_addr(yc))
        hlo_t = nc.alloc_sbuf_tensor_at("hlo", [128, 8, NT], F32R, offset=_addr(hT))

        hactlo_t = nc.alloc_sbuf_tensor_at("hactlo", [128, NFF, NT], F32R, offset=_addr(big))
        acch_t = [nc.alloc_sbuf_tensor_at("acch%d" % i, [128, NT], F32R, offset=_addr(acc[i][0])) for i in range(4)]
        zDh_t = nc.alloc_sbuf_tensor_at("zDh", [128, 2, 2 + NT], F32R, offset=_addr(zD))
        HACT_HI = ([(acch_t[i][:, :], acc[i][1]) for i in range(4)] + [(zDh_t[:, c, 0:NT], zD_r0) for c in range(2)]
                   + [(hhi_t[:, c, :], yc_r[c]) for c in range(8)] + [(hlo_t[:, c, :], hT_r[c]) for c in range(8)])

        def wlo(i):
            return wlo_t[i][:, :].rearrange("p (k c) -> p k c", k=8)

        def whi(i):
            return whi_t[i][:, :].rearrange("p (k c) -> p k c", k=8)

        wseq = []

        def layer_loads(l, split, with_ada1):
            seq = []
            for a in range(8):
                seq.append(("in", l, a, split))
                if with_ada1:
                    seq.append(("ada", 0, 4 + a, False))
            for a in range(4):
                seq.append(("out", l, a, split))
            for cg in range(11):
                seq.append(("gate", l, cg, split))
                seq.append(("up", l, cg, split))
                if with_ada1:
                    seq.append(("ada", 1, cg, False))
            if with_ada1:
                seq.append(("ada", 1, 11, False))
            if split:
                for a in range(16):
                    seq.append(("downh", l, a, True))
            else:
                for ob in range(8):
                    seq.append(("down", l, ob, False))
            return seq

        for j in range(4):
            wseq.append(("ada", 0, j, False))
        for t in range(NTILES + 1):
            for l in range(2):
                wseq.extend(layer_loads(l, t < NTILES, t == 0 and l == 0))
        wpos = [0]
        wiss = [0]
        wsplit_done = [0]

        def w_issue(j):
            kind, l, a, _ = wseq[j]
            s = j % NSLOT
            if kind == "ada":
                o = wview8(s, 512); i = w_ada[l][:, 512 * a:512 * a + 512].rearrange("(k p) c -> p k c", p=128)
            elif kind == "in":
                o = wraw(s); i = w_in[l][:, 256 * a:256 * a + 256].rearrange("(k p) c -> p k c", p=128)
            elif kind == "out":
                o = wraw(s); i = w_out[l][:, 256 * a:256 * a + 256].rearrange("(k p) c -> p k c", p=128)
            elif kind in ("gate", "up"):
                src = w_gate if kind == "gate" else w_up
                o = wraw(s); i = src[l][:, 256 * a:256 * a + 256].rearrange("(k p) c -> p k c", p=128)
            elif kind == "downh":
                ob_, hf_ = divmod(a, 2)
                o = wsl[s][:, 0:1408].rearrange("p (k c) -> p k c", k=11)
                i = w_down[l][1408 * hf_:1408 * hf_ + 1408, 128 * ob_:128 * ob_ + 128].rearrange("(k p) c -> p k c", p=128)
            else:
                o = wview22(s); i = w_down[l][:, 128 * a:128 * a + 128].rearrange("(k p) c -> p k c", p=128)
            P.dma(sp, wsl_d[s], [(o, i)], writes=[wsl_r[s]])

        def w_split(j):
            if not wseq[j][3]:
                return
            s = j % NSLOT
            nw = 1408 if wseq[j][0] == "downh" else 2048
            raw = wsl[s][:, 0:nw]
            P.op(act, lambda: S.copy(out=whi_t[s][:, 0:nw], in_=raw), reads=[wsl_r[s]], writes=[wsl_r[s]])
            P.op(dve, lambda: V.tensor_tensor(out=wlo_t[s][:, 0:nw], in0=raw, in1=whi_t[s][:, 0:nw].bitcast(F32), op=ALU.subtract),
                 reads=[wsl_r[s]], writes=[wsl_r[s]])

        def wnext(kind, l, a):
            j = wpos[0]
            assert wseq[j][:3] == (kind, l, a), (wseq[j], kind, l, a)
            while wiss[0] < min(j + NSLOT, len(wseq)):
                w_issue(wiss[0])
                wiss[0] += 1
            while wsplit_done[0] < min(j + 2, wiss[0]):
                w_split(wsplit_done[0])
                wsplit_done[0] += 1
            wpos[0] += 1
            s = j % NSLOT
            return s, wsl_r[s]

        P.dma(pool, d_misc, [(ident[:], c_ident), (tril[:], c_tril), (invw[:], c_invw), (invcnt[:], c_invcnt)],
              writes=[ident_r, tril_r, invw_r, invcnt_r])
        P.op(dve, lambda: V.memset(ones[:], 1.0), writes=[ones_r])

        def mm_group(out_ap, out_reg, pairs, reads):
            n = len(pairs)
            for i, (l_ap, r_ap) in enumerate(pairs):
                P.op(pe, (lambda l_ap=l_ap, r_ap=r_ap, i=i: T.matmul(out_ap, l_ap, r_ap, start=(i == 0), stop=(i == n - 1))),
                     reads=reads, writes=[out_reg], inc=(i == n - 1))

        def transpose(out_ap, out_reg, in_ap, in_regs, npart, inc=True):
            P.op(pe, lambda: T.transpose(out_ap, in_ap, ident[0:npart, 0:npart]),
                 reads=list(in_regs) + [ident_r], writes=[out_reg], inc=inc)

        for l in range(2):
            st0, st0_r = stg[0]
            st1, st1_r = stg[1]

            def rows(v):
                return v.rearrange("(c p) -> c p", p=128)

            pairs = [
                (st0[C_BADA:C_BADA + 48, 0:128], rows(b_ada[l])),
                (st0[C_GPM:C_GPM + 8, 0:128], rows(g_pre_mix[l])),
                (st0[C_GOM:C_GOM + 8, 0:128], rows(g_post_mix[l])),
                (st0[C_GPF:C_GPF + 8, 0:128], rows(g_pre_ffn[l])),
                (st0[C_GOF:C_GOF + 8, 0:128], rows(g_post_ffn[l])),
                (st0[C_CAB:C_CAB + 2, 0:128], rows(conv_a_b[l])),
                (st0[C_LAG:C_LAG + 2, 0:128], rows(ln_a_g[l])),
                (st0[C_LAB:C_LAB + 2, 0:128], rows(ln_a_b[l])),
                (st0[C_PSC:C_PSC + 2, 0:128], rows(pool_scale[l])),
            ]
            for h in range(2):
                pairs.append((st0[C_CDW + 3 * h:C_CDW + 3 * h + 3, 0:128], conv_d_w[l][:, 128 * h:128 * h + 128]))
            P.dma(pool, stg_d[0], pairs, writes=[st0_r])
            pairs = []
            for h in range(2):
                pairs.append((st1[31 * h:31 * h + 31, 0:128], conv_a_w[l][:, 128 * h:128 * h + 128]))
            P.dma(pool, stg_d[1], pairs, writes=[st1_r])
            bk, bk_r = nb()
            transpose(bk[:, 0:NS1], bk_r, st0[0:NS1, 0:128], [st0_r], NS1, inc=False)
            transpose(bk[:, NS1:NS1 + 62], bk_r, st1[0:62, 0:128], [st1_r], 62)
            pT, pT_r = parT[l]
            P.op(dve, lambda: V.tensor_copy(pT[:, 0:NPAR], bk[:, 0:NPAR]), reads=[bk_r], writes=[pT_r])
            g_, g_r = gcb[l]
            b_, b_r = bsb[l]
            w0_, w0_r = w00[l]
            pairs = [(g_[:, 0, :], ln_c_g[l].partition_broadcast(128)), (g_[:, 1, :], ln_c_b[l].partition_broadcast(128))]
            for h in range(4):
                rs_ = slice(64 * (h % 2), 64 * (h % 2) + 64)
                pairs.append((b_[rs_, h // 2, :], sgu_b[l, h].partition_broadcast(64)))
                pairs.append((w0_[rs_, h // 2:h // 2 + 1], sgu_w[l, h, 0, 0:1].partition_broadcast(64)))
            P.dma(pool, P.dsem("par_a%d" % l), pairs, writes=[g_r, b_r, w0_r])
            bd_, bd_r = BD[l]
            P.op(dve, lambda: V.memset(bd_[:], 0.0), writes=[bd_r])
            pairs = []
            for g in range(4):
                r0 = 64 * (g % 2)
                pairs.append((bd_[r0:r0 + 64, g // 2, r0:r0 + 64], pool_w[l, g]))
            P.dma(pool, P.dsem("par_b%d" % l), pairs, writes=[bd_r])
            wm_, wm_r = wmT[l]
            for h in range(4):
                s_, s_r = stg[h % 2]
                P.dma(pool, stg_d[h % 2], [(s_[:, 0:128], sgu_w[l, h])], writes=[s_r])
                P.op(dve, lambda: V.tensor_tensor(out=s_[:, 0:128], in0=s_[:, 0:128], in1=tril[:], op=ALU.mult),
                     reads=[s_r, tril_r], writes=[s_r])
                bk, bk_r = nb()
                transpose(bk[:, 0:128], bk_r, s_[:, 0:128], [s_r], 128)
                P.op(dve, lambda: V.tensor_copy(wm_[:, h, :], bk[:, 0:128]), reads=[bk_r], writes=[wm_r])

        s_, s_r = stg[0]
        P.dma(pool, stg_d[0], [(s_[0:NS + 1, :], cc)], writes=[s_r])
        P.op(act, lambda: S.activation(out=s_[0:NS + 1, :], in_=s_[0:NS + 1, :], func=AF.Silu), reads=[s_r], writes=[s_r])
        bk, bk_r = nb()
        for dc in range(8):
            transpose(bk[:, dc * (NS + 1):(dc + 1) * (NS + 1)], bk_r, s_[0:NS + 1, dc * 128:(dc + 1) * 128], [s_r], NS + 1, inc=(dc == 7))
        P.op(dve, lambda: V.tensor_copy(scT[:].rearrange("p a b -> p (a b)"), bk[:, 0:8 * (NS + 1)]), reads=[bk_r], writes=[scT_r])
        NC1 = NS + 1

        def mod_load(l, j):
            mT_, mT_r = modT[l]
            pT, pT_r = parT[l]
            if True:
                s, s_r = wnext("ada", l, j)
                wv = wview8(s, 512)
                bkt, bkt_r = nb()
                mm_group(bkt[0:NC1, 0:512], bkt_r, [(scT[:, kc, :], wv[:, kc, :]) for kc in range(8)], reads=[s_r, scT_r])
                tk, tk_r = tm[j % 2]
                P.op(act, lambda: S.copy(out=tk[0:NC1, 0:512], in_=bkt[0:NC1, 0:512]), reads=[bkt_r], writes=[tk_r])
                bk, bk_r = nb()
                for q in range(4):
                    transpose(bk[:, q * NC1:(q + 1) * NC1], bk_r, tk[0:NC1, q * 128:(q + 1) * 128], [tk_r], NC1, inc=(q == 3))
                fc0 = 4 * j
                P.op(dve, lambda: V.tensor_tensor(out=mT_[:, fc0:fc0 + 4, :],
                                                  in0=bk[:, 0:4 * NC1].rearrange("p (a b) -> p a b", a=4),
                                                  in1=pT[:, C_BADA + fc0:C_BADA + fc0 + 4].unsqueeze(2).broadcast_to([128, 4, NC1]),
                                                  op=ALU.add),
                     reads=[bk_r, pT_r], writes=[mT_r])

        def mod_finish(l, which=("A1", "G1", "A2", "G2")):
            mT_, mT_r = modT[l]
            pT, pT_r = parT[l]
            for dc in range(8):
                if "A1" in which:
                    P.op(dve, lambda: V.tensor_scalar(out=A1[l][0][:, dc, :], in0=mT_[:, 8 + dc, :], scalar1=1.0, scalar2=pT[:, C_GPM + dc:C_GPM + dc + 1],
                                                      op0=ALU.add, op1=ALU.mult), reads=[mT_r, pT_r], writes=[A1[l][1]])
                if "G1" in which:
                    P.op(dve, lambda: V.tensor_scalar(out=G1[l][0][:, dc, :], in0=mT_[:, 16 + dc, :], scalar1=pT[:, C_GOM + dc:C_GOM + dc + 1], scalar2=None,
                                                      op0=ALU.mult), reads=[mT_r, pT_r], writes=[G1[l][1]])
                if "A2" in which:
                    P.op(dve, lambda: V.tensor_scalar(out=A2[l][0][:, dc, :], in0=mT_[:, 32 + dc, :], scalar1=1.0, scalar2=pT[:, C_GPF + dc:C_GPF + dc + 1],
                                                      op0=ALU.add, op1=ALU.mult), reads=[mT_r, pT_r], writes=[A2[l][1]])
                if "G2" in which:
                    P.op(dve, lambda: V.tensor_scalar(out=G2[l][0][:, dc, :], in0=mT_[:, 40 + dc, :], scalar1=pT[:, C_GOF + dc:C_GOF + dc + 1], scalar2=None,
                                                      op0=ALU.mult), reads=[mT_r, pT_r], writes=[G2[l][1]])

        for j in range(4):
            mod_load(0, j)
        mod_finish(0, ("A1",))

        def sample_stats(src8, src_regs, eps, dim):
            q, q_r = sq[0]
            qv = q[:, 0:8 * NS].rearrange("p (c n) -> p c n", c=8)
            P.op(act, lambda: S.activation(out=qv, in_=src8, func=AF.Square), reads=list(src_regs), writes=[q_r])
            bk, bk_r = nb()
            P.op(pe, lambda: T.matmul(bk[:, 0:8 * NS], ones[:], q[:, 0:8 * NS], start=True, stop=True), reads=[ones_r, q_r], writes=[bk_r], inc=True)
            P.op(dve, lambda: V.tensor_reduce(out=ssum[:, 0:NS], in_=bk[:, 0:8 * NS].rearrange("p (c n) -> p n c", c=8), axis=AX.X, op=ALU.add),
                 reads=[bk_r], writes=[ssum_r])
            P.op(act, lambda: S.activation(out=rstd[:, :NS], in_=ssum[:, 0:NS], func=AF.Sqrt, scale=1.0 / dim, bias=eps), reads=[ssum_r], writes=[rstd_r])
            P.op(dve, lambda: V.reciprocal(out=rstd[:, :NS], in_=rstd[:, :NS]), reads=[rstd_r], writes=[rstd_r])

        def rms_stats(src, src_regs, n, N, eps, dim):
            if N == NS and n == 8:
                sample_stats(src[:, 0:8, 0:NS], src_regs[0:8], eps, dim)
                return
            st_ = StatAccPE(N)
            for c in range(n):
                st_.square(src[:, c, :N], src_regs[c], "pool")
                st_.mm(last=(c == n - 1))
            st_.finish(eps, dim)

        class StatAccPE:
            def __init__(self, N, bank=None):
                if bank is None:
                    self.bk, self.bk_r = nb(hold=True)
                else:
                    self.bk, self.bk_r = bank
                    held.add(pb.index(self.bk))
                self.N = N
                self.k = 0
                self.n = 0
                self.pending = []

            def square(self, src_ap, src_reg, eng):
                q, q_r = sq[self.k % 2]
                self.k += 1
                N = self.N
                if eng == "pool":
                    P.op(pool, lambda: G.tensor_tensor(out=q[:, :N], in0=src_ap, in1=src_ap, op=ALU.mult), reads=[src_reg], writes=[q_r])
                else:
                    P.op(act, lambda: S.activation(out=q[:, :N], in_=src_ap, func=AF.Square), reads=[src_reg], writes=[q_r])
                self.pending.append((q, q_r))

            def mm(self, last=False):
                q, q_r = self.pending.pop(0)
                N = self.N
                first = (self.n == 0)
                self.n += 1
                P.op(pe, lambda: T.matmul(self.bk[:, :N], ones[:], q[:, :N], start=first, stop=last), reads=[ones_r, q_r], writes=[self.bk_r], inc=True)

            def finish(self, eps, dim):
                N = self.N
                assert not self.pending
                held.discard(pb.index(self.bk))
                P.op(act, lambda: S.activation(out=rstd[:, :N], in_=self.bk[:, :N], func=AF.Sqrt, scale=1.0 / dim, bias=eps),
                     reads=[self.bk_r], writes=[rstd_r])
                P.op(dve, lambda: V.reciprocal(out=rstd[:, :N], in_=rstd[:, :N]), reads=[rstd_r], writes=[rstd_r])

        class StatAcc:
            def __init__(self, N):
                self.N = N
                self.k = 0

            def square(self, src_ap, src_reg, eng):
                N = self.N
                first = (self.k == 0)
                q, q_r = (ssum, ssum_r) if first else sq[self.k % 2]
                self.k += 1
                if eng == "pool":
                    P.op(pool, lambda: G.tensor_tensor(out=q[:, :N], in0=src_ap, in1=src_ap, op=ALU.mult), reads=[src_reg], writes=[q_r])
                else:
                    P.op(act, lambda: S.activation(out=q[:, :N], in_=src_ap, func=AF.Square), reads=[src_reg], writes=[q_r])
                if not first:
                    P.op(pool, lambda: G.tensor_tensor(out=ssum[:, :N], in0=ssum[:, :N], in1=q[:, :N], op=ALU.add), reads=[ssum_r, q_r], writes=[ssum_r])

            def mm(self, last=False):
                pass

            def finish(self, eps, dim):
                N = self.N
                bk, bk_r = nb()
                P.op(pe, lambda: T.matmul(bk[:, :N], ones[:], ssum[:, :N], start=True, stop=True), reads=[ones_r, ssum_r], writes=[bk_r], inc=True)
                P.op(act, lambda: S.activation(out=rstd[:, :N], in_=bk[:, :N], func=AF.Sqrt, scale=1.0 / dim, bias=eps),
                     reads=[bk_r], writes=[rstd_r])
                P.op(dve, lambda: V.reciprocal(out=rstd[:, :N], in_=rstd[:, :N]), reads=[rstd_r], writes=[rstd_r])

        def modulate(l, Aq, Bbase, N, sample):
            A_, A_r = Aq
            mT_, mT_r = modT[l]
            if not sample:
                def s1(c):
                    t_, t_r = tm[c % 3]
                    P.op(dve, lambda: V.scalar_tensor_tensor(out=t_[:, :N], in0=xT[:, c, :N], scalar=A_[:, c, 0:1], in1=rstd[:, :N],
                                                             op0=ALU.mult, op1=ALU.mult),
                         reads=[xT_r[c], A_r, rstd_r], writes=[t_r])
                    P.op(act, lambda: S.activation(out=hhi_t[:, c, :N], in_=t_[:, :N], func=AF.Identity, bias=mT_[:, Bbase + c, 0:1], scale=1.0),
                         reads=[t_r, mT_r], writes=[yc_r[c]])

                def s2(c):
                    t_, t_r = tm[c % 3]
                    P.op(dve, lambda: V.scalar_tensor_tensor(out=hlo_t[:, c, :N], in0=t_[:, :N], scalar=mT_[:, Bbase + c, 0:1],
                                                             in1=hhi_t[:, c, :N].bitcast(F32), op0=ALU.add, op1=ALU.subtract),
                         reads=[t_r, mT_r, yc_r[c]], writes=[hT_r[c]])
                s1(0)
                for c in range(8):
                    if c + 1 < 8:
                        s1(c + 1)
                    s2(c)
                return
            t_, t_r = tm[0]
            tv = t_[:, 0:8 * NS].rearrange("p (c n) -> p c n", c=8)
            rb = rstd[:, 0:NS].unsqueeze(1).broadcast_to([128, 8, NS])
            P.op(dve, lambda: V.tensor_tensor(out=tv, in0=xT[:, :, 0:NS], in1=rb, op=ALU.mult), reads=xT_r + [rstd_r], writes=[t_r])
            P.op(dve, lambda: V.tensor_tensor(out=tv, in0=tv, in1=A_[:, :, 1:1 + NS], op=ALU.mult), reads=[t_r, A_r], writes=[t_r])
            P.op(dve, lambda: V.tensor_tensor(out=hT[:, :, 0:NS], in0=tv, in1=mT_[:, Bbase:Bbase + 8, 1:1 + NS], op=ALU.add),
                 reads=[t_r, mT_r], writes=hT_r)

        def resid_update(l, src, src_regs, Gq, N, sample, want_stats=False, src_aps=None, stat_bank=None):
            G_, G_r = Gq
            if sample:
                t_, t_r = tm[0]
                tv = t_[:, 0:8 * NS].rearrange("p (c n) -> p c n", c=8)
                rb = rstd[:, 0:NS].unsqueeze(1).broadcast_to([128, 8, NS])
                P.op(dve, lambda: V.tensor_tensor(out=tv, in0=src[:, 0:8, 0:NS], in1=rb, op=ALU.mult), reads=list(src_regs[0:8]) + [rstd_r], writes=[t_r])
                P.op(dve, lambda: V.tensor_tensor(out=tv, in0=tv, in1=G_[:, :, 1:1 + NS], op=ALU.mult), reads=[t_r, G_r], writes=[t_r])
                P.op(dve, lambda: V.tensor_tensor(out=xT[:, :, 0:NS], in0=xT[:, :, 0:NS], in1=tv, op=ALU.add), reads=xT_r + [t_r], writes=xT_r)
                if want_stats:
                    sample_stats(xT[:, :, 0:NS], xT_r, RMS_EPS, D)
                return
            st2 = StatAccPE(N, bank=stat_bank) if want_stats else None
            for c in range(8):
                t_, t_r = tm[c % 2]
                if not sample:
                    sap = src_aps[c] if src_aps is not None else src[:, c, :N]
                    P.op(dve, lambda: V.scalar_tensor_tensor(out=t_[:, :N], in0=sap, scalar=G_[:, c, 0:1], in1=rstd[:, :N],
                                                             op0=ALU.mult, op1=ALU.mult),
                         reads=[src_regs[c], G_r, rstd_r], writes=[t_r])
                else:
                    P.op(dve, lambda: V.tensor_tensor(out=t_[:, :N], in0=src[:, c, :N], in1=rstd[:, :N], op=ALU.mult),
                         reads=[src_regs[c], rstd_r], writes=[t_r])
                    P.op(dve, lambda: V.tensor_tensor(out=t_[:, :N], in0=t_[:, :N], in1=G_[:, c, 1:1 + N], op=ALU.mult),
                         reads=[t_r, G_r], writes=[t_r])
                P.op(pool, lambda: G.tensor_tensor(out=xT[:, c, :N], in0=xT[:, c, :N], in1=t_[:, :N], op=ALU.add),
                     reads=[xT_r[c], t_r], writes=[xT_r[c]])
                if st2 is not None:
                    st2.square(xT[:, c, :N], xT_r[c], "act")
                    st2.mm(last=(c == 7))
            if st2 is not None:
                st2.finish(RMS_EPS, D)

        def small_out(dst_ap, chunks, nrow):
            k = smi[0] % 2
            smi[0] += 1
            s_, s_r = sm[k]
            bk, bk_r = nb()
            for ch, (ap_, regs) in enumerate(chunks):
                transpose(bk[0:nrow, ch * 128:(ch + 1) * 128], bk_r, ap_, regs, 128, inc=(ch == 1))
            P.op(dve, lambda: V.tensor_copy(s_[0:nrow, 0:256], bk[0:nrow, 0:256]), reads=[bk_r], writes=[s_r])
            P.dma(pool, sm_d[k], [(dst_ap, s_[0:nrow, 0:256])], reads=[s_r], is_output=True)

        tki = [0]

        def proj_blocks(wv, s_r, blocks, src, src_regs, nk, N, sample, tok_copy=None, split=None, kc_major=False, kc_order=None):
            blocks = list(blocks)
            if not sample and kc_major:
                bks = [nb() for _ in blocks]
                order = list(kc_order) if kc_order is not None else list(range(nk))
                for oi, kc in enumerate(order):
                    for bi, j in enumerate(blocks):
                        bk, bk_r = bks[bi]
                        cs = slice(j * 128, (j + 1) * 128)
                        if split is not None:
                            w_h, w_l, x_h, x_l, x_regs = split
                            trip = [(w_h[:, kc, cs], x_h[:, kc, :N]), (w_h[:, kc, cs], x_l[:, kc, :N]), (w_l[:, kc, cs], x_h[:, kc, :N])]
                            rr = [s_r, x_regs[kc], x_regs[8 + kc]]
                        else:
                            trip = [(wv[:, kc, cs], src[:, kc, :N])]
                            rr = [s_r, src_regs[kc]]
                        for ti_, (l_ap, r_ap) in enumerate(trip):
                            first = (oi == 0 and ti_ == 0)
                            lastm = (oi == nk - 1 and ti_ == len(trip) - 1)
                            P.op(pe, (lambda l_ap=l_ap, r_ap=r_ap, first=first, lastm=lastm, bk=bk: T.matmul(bk[:, :N], l_ap, r_ap, start=first, stop=lastm)),
                                 reads=rr, writes=[bk_r], inc=(ti_ == len(trip) - 1))
                for bi, j in enumerate(blocks):
                    yield j, bks[bi][0][:, :N], bks[bi][1]
                return
            if not sample:
                for j in blocks:
                    bk, bk_r = nb()
                    if split is None:
                        mm_group(bk[:, :N], bk_r, [(wv[:, kc, j * 128:(j + 1) * 128], src[:, kc, :N]) for kc in range(nk)],
                                 reads=[s_r] + src_regs)
                    else:
                        w_h, w_l, x_h, x_l, x_regs = split
                        pairs = []
                        for kc in range(nk):
                            cs = slice(j * 128, (j + 1) * 128)
                            pairs += [(w_h[:, kc, cs], x_h[:, kc, :N]), (w_h[:, kc, cs], x_l[:, kc, :N]), (w_l[:, kc, cs], x_h[:, kc, :N])]
                        mm_group(bk[:, :N], bk_r, pairs, reads=[s_r] + x_regs)
                    yield j, bk[:, :N], bk_r
                return
            c0 = 0 if tok_copy is not None else min(blocks) * 128
            c1 = (max(blocks) + 1) * 128 if blocks else 256
            bk, bk_r = nb()
            mm_group(bk[0:NS, c0:c1], bk_r, [(src[:, kc, 0:NS], wv[:, kc, c0:c1]) for kc in range(nk)], reads=[s_r] + src_regs)
            tk, tk_r = tm[tki[0] % 2]
            tki[0] += 1
            P.op(act, lambda: S.copy(out=tk[0:NS, c0:c1], in_=bk[0:NS, c0:c1]), reads=[bk_r], writes=[tk_r])
            if tok_copy is not None:
                tok_copy(tk, tk_r)
            if not blocks:
                return
            bk2, bk2_r = nb()
            for j in blocks:
                transpose(bk2[:, j * NS:(j + 1) * NS], bk2_r, tk[0:NS, j * 128:(j + 1) * 128], [tk_r], NS, inc=(j == blocks[-1]))
            for j in blocks:
                yield j, bk2[:, j * NS:(j + 1) * NS], bk2_r

        def block(l, ti, sample, have_stats):
            N = NS if sample else NT
            first = (ti == 0)
            last = (ti == NTILES - 1)
            pT, pT_r = parT[l]

            def pcol(c):
                return pT[:, c:c + 1]

            dB = [(big[:, 4, :], big_r[4]), (big[:, 5, :], big_r[5])]
            ntb = 1 if sample else 4
            mt = NS if sample else 128

            if not have_stats:
                rms_stats(xT, xT_r, 8, N, RMS_EPS, D)
            modulate(l, A1[l], 0, N, sample)

            h_hi = hhi_t[:]
            h_lo = hlo_t[:]

            def inproj(a):
                s, s_r = wnext("in", l, a)
                wv = wraw(s)
                sp_ = None if sample else (whi(s), wlo(s), h_hi, h_lo, hT_r + yc_r)
                blocks = [0, 1]
                tok_copy = None
                if a == 4:
                    blocks = []
                    if not sample:
                        w_h, w_l = whi(s), wlo(s)
                        for tb in range(4):
                            bk, bk_r = nb()
                            ts_ = slice(tb * 128, tb * 128 + 128)
                            pairs = []
                            for kc in range(8):
                                pairs += [(h_hi[:, kc, ts_], w_h[:, kc, :]), (h_lo[:, kc, ts_], w_h[:, kc, :]), (h_hi[:, kc, ts_], w_l[:, kc, :])]
                            mm_group(bk[:, 0:256], bk_r, pairs, reads=[s_r] + hT_r + yc_r)
                            P.op(act, lambda: S.copy(out=cv[:, tb, :], in_=bk[:, 0:256]), reads=[bk_r], writes=cv_r)
                        return
                    else:
                        def tok_copy(tk, tk_r):
                            P.op(pool, lambda: G.tensor_copy(out=cv[0:NS, 0, :], in_=tk[0:NS, 0:256]), reads=[tk_r], writes=cv_r)
                for j, ps, ps_r in proj_blocks(wv, s_r, blocks, hT, hT_r, 8, N, sample, tok_copy=tok_copy, split=sp_, kc_major=(a == 0)):
                    cb = 2 * a + j
                    if cb in (2, 3):
                        P.op(act, lambda: S.activation(out=big[:, cb, :N], in_=ps, func=AF.Sigmoid), reads=[ps_r], writes=[big_r[cb]])
                    elif cb in (4, 5):
                        P.op(act, lambda: S.copy(out=zB[:, cb - 4, 15:15 + N], in_=ps), reads=[ps_r], writes=zB_r)
                    else:
                        P.op(act, lambda: S.copy(out=big[:, cb, :N], in_=ps), reads=[ps_r], writes=[big_r[cb]])

            def mixA1():
                if sample:
                    yield
                hA, hA_r = haloA[l]
                for ch in range(2):
                    P.op(dve, lambda: V.tensor_tensor(out=zA[:, ch, 30:30 + N], in0=big[:, ch, :N], in1=big[:, 2 + ch, :N], op=ALU.mult),
                         reads=[big_r[ch], big_r[2 + ch]], writes=zA_r)
                if not sample:
                    if first:
                        P.op(pool, lambda: G.memset(zA[:, :, 0:30], 0.0), writes=zA_r)
                    else:
                        P.op(pool, lambda: G.tensor_copy(out=zA[:, :, 0:30], in_=hA[:]), reads=[hA_r], writes=zA_r)
                    for k in range(31):
                        if k in (8, 16, 24):
                            yield
                        for ch in range(2):
                            a_, a_r = acc[2 * ch + (k % 2)]
                            wc = pcol(C_CAW + 31 * ch + k)
                            if k < 2:
                                if k == 0:
                                    P.op(dve, lambda: V.tensor_scalar(out=a_[:, :N], in0=zA[:, ch, k:k + N], scalar1=wc, scalar2=pcol(C_CAB + ch),
                                                                      op0=ALU.mult, op1=ALU.add), reads=zA_r + [pT_r], writes=[a_r])
                                else:
                                    P.op(dve, lambda: V.tensor_scalar(out=a_[:, :N], in0=zA[:, ch, k:k + N], scalar1=wc, scalar2=None,
                                                                      op0=ALU.mult), reads=zA_r + [pT_r], writes=[a_r])
                            else:
                                P.op(dve, lambda: V.scalar_tensor_tensor(out=a_[:, :N], in0=zA[:, ch, k:k + N], scalar=wc, in1=a_[:, :N],
                                                                         op0=ALU.mult, op1=ALU.add), reads=zA_r + [pT_r, a_r], writes=[a_r])
                    for ch in range(2):
                        P.op(pool, lambda: G.tensor_tensor(out=acc[2 * ch][0][:, :N], in0=acc[2 * ch][0][:, :N], in1=acc[2 * ch + 1][0][:, :N], op=ALU.add),
                             reads=[acc[2 * ch][1], acc[2 * ch + 1][1]], writes=[acc[2 * ch][1]])
                    if last:
                        small_out(nap[l], [(zA[:, ch, N:N + 30], zA_r) for ch in range(2)], 30)
                    else:
                        P.op(pool, lambda: G.tensor_copy(out=hA[:], in_=zA[:, :, N:N + 30]), reads=zA_r, writes=[hA_r])
                else:
                    for ch in range(2):
                        a_, a_r = acc[2 * ch]
                        P.op(dve, lambda: V.tensor_tensor(out=prd[:].rearrange("p (s k) -> p s k", s=NS),
                                                          in0=stAT[:, ch, :].rearrange("p (s k) -> p s k", s=NS),
                                                          in1=pT[:, C_CAW + 31 * ch:C_CAW + 31 * ch + 30].unsqueeze(1).broadcast_to([128, NS, 30]),
                                                          op=ALU.mult), reads=[stAT_r, pT_r], writes=[prd_r])
                        P.op(dve, lambda: V.tensor_reduce(out=red[:, ch, :], in_=prd[:].rearrange("p (s k) -> p s k", s=NS), axis=AX.X, op=ALU.add),
                             reads=[prd_r], writes=[red_r])
                        P.op(dve, lambda: V.scalar_tensor_tensor(out=a_[:, :N], in0=zA[:, ch, 30:30 + N], scalar=pcol(C_CAW + 31 * ch + 30), in1=red[:, ch, :],
                                                                 op0=ALU.mult, op1=ALU.add), reads=zA_r + [pT_r, red_r], writes=[a_r])
                        P.op(dve, lambda: V.tensor_scalar(out=a_[:, :N], in0=a_[:, :N], scalar1=pcol(C_CAB + ch), scalar2=None, op0=ALU.add),
                             reads=[a_r, pT_r], writes=[a_r])
                    small_out(nas[l, :, 29, :], [(zA[:, ch, 30:30 + N], zA_r) for ch in range(2)], NS)

            def mixA2a():
                bk, bk_r = nb()
                mm_group(bk[:, :N], bk_r, [(ones[:], acc[2 * ch][0][:, :N]) for ch in range(2)], reads=[ones_r, acc[0][1], acc[2][1]])
                for ch in range(2):
                    a_, a_r = acc[2 * ch]
                    P.op(dve, lambda: V.scalar_tensor_tensor(out=a_[:, :N], in0=bk[:, :N], scalar=-1.0 / 256, in1=a_[:, :N], op0=ALU.mult, op1=ALU.add),
                         reads=[bk_r, a_r], writes=[a_r])
                    P.op(pool, lambda: G.tensor_tensor(out=acc[2 * ch + 1][0][:, :N], in0=a_[:, :N], in1=a_[:, :N], op=ALU.mult),
                         reads=[a_r], writes=[acc[2 * ch + 1][1]])

            def mixA2b():
                bk2, bk2_r = nb()
                mm_group(bk2[:, :N], bk2_r, [(ones[:], acc[2 * ch + 1][0][:, :N]) for ch in range(2)], reads=[ones_r, acc[1][1], acc[3][1]])
                t0, t0_r = tm[2]
                P.op(act, lambda: S.activation(out=t0[:, :N], in_=bk2[:, :N], func=AF.Sqrt, scale=1.0 / 256, bias=LN_EPS), reads=[bk2_r], writes=[t0_r])
                P.op(dve, lambda: V.reciprocal(out=t0[:, :N], in_=t0[:, :N]), reads=[t0_r], writes=[t0_r])
                for ch in range(2):
                    a_, a_r = acc[2 * ch]
                    P.op(dve, lambda: V.tensor_tensor(out=a_[:, :N], in0=a_[:, :N], in1=t0[:, :N], op=ALU.mult), reads=[a_r, t0_r], writes=[a_r])
                    P.op(act, lambda: S.activation(out=yc[:, ch, :N], in_=a_[:, :N], func=AF.Silu, scale=pcol(C_LAG + ch), bias=pcol(C_LAB + ch)),
                         reads=[a_r, pT_r], writes=[yc_r[ch]])


            def mixB1():
                hB, hB_r = haloB[l]
                dB = [(big[:, 4, :], big_r[4]), (big[:, 5, :], big_r[5])]
                if not sample:
                    E = 15 + N
                    if first:
                        P.op(pool, lambda: G.memset(zB[:, :, 0:15], 0.0), writes=zB_r)
                    else:
                        P.op(pool, lambda: G.tensor_copy(out=zB[:, :, 0:15], in_=hB[:]), reads=[hB_r], writes=zB_r)
                    (P0, P0_r), (Q0, Q0_r), (P1, P1_r), (Q1, Q1_r) = PQ

                    def shadd(o, o_r, i, i_r, sh, lo, rows):
                        P.op(dve, lambda: V.tensor_tensor(out=o[rows, lo:E], in0=i[rows, lo:E], in1=i[rows, lo - sh:E - sh], op=ALU.add),
                             reads=[i_r], writes=[o_r])
                    al = slice(0, 128); hi = slice(64, 128); lo_ = slice(0, 64)
                    P.op(dve, lambda: V.tensor_tensor(out=P0[:, 1:E], in0=zB[:, 0, 1:E], in1=zB[:, 0, 0:E - 1], op=ALU.add), reads=zB_r, writes=[P0_r])
                    shadd(Q0, Q0_r, P0, P0_r, 2, 3, hi)
                    P.op(dve, lambda: V.tensor_tensor(out=P1[:, 1:E], in0=zB[:, 1, 1:E], in1=zB[:, 1, 0:E - 1], op=ALU.add), reads=zB_r, writes=[P1_r])
                    shadd(Q1, Q1_r, P1, P1_r, 2, 3, al)
                    shadd(P1, P1_r, Q1, Q1_r, 4, 7, al)
                    shadd(Q1, Q1_r, P1, P1_r, 8, 15, hi)
                    srcs = [[(P0, P0_r, lo_), (Q0, Q0_r, hi)], [(P1, P1_r, lo_), (Q1, Q1_r, hi)]]
                    for ch in range(2):
                        d_, d_r = dB[ch]
                        for (s_, s_r, rows) in srcs[ch]:
                            P.op(dve, lambda: V.scalar_tensor_tensor(out=d_[rows, :N], in0=s_[rows, 15:E], scalar=invw[rows, ch:ch + 1], in1=zB[rows, ch, 15:E],
                                                                     op0=ALU.mult, op1=ALU.subtract), reads=[s_r, invw_r] + zB_r, writes=[d_r])
                            if first:
                                t1, t1_r = tm[2]
                                P.op(dve, lambda: V.tensor_tensor(out=t1[rows, 0:16], in0=s_[rows, 15:31], in1=invcnt[rows, ch, :], op=ALU.mult),
                                     reads=[s_r, invcnt_r], writes=[t1_r])
                                P.op(dve, lambda: V.tensor_tensor(out=d_[rows, 0:16], in0=t1[rows, 0:16], in1=zB[rows, ch, 15:31], op=ALU.subtract),
                                     reads=[t1_r, d_r] + zB_r, writes=[d_r])
                    if last:
                        small_out(nbp[l], [(zB[:, ch, N:N + 15], zB_r) for ch in range(2)], 15)
                    else:
                        P.op(pool, lambda: G.tensor_copy(out=hB[:], in_=zB[:, :, N:N + 15]), reads=zB_r, writes=[hB_r])
                else:
                    for ch in range(2):
                        d_, d_r = dB[ch]
                        for half in range(2):
                            rows = slice(64 * half, 64 * half + 64)
                            w = POOLW[2 * ch + half]
                            k0 = 15 - (w - 1)
                            P.op(dve, lambda: V.tensor_reduce(out=red[rows, ch, :], in_=stBT[rows, ch, :].rearrange("p (s k) -> p s k", s=NS)[:, :, k0:15],
                                                              axis=AX.X, op=ALU.add), reads=[stBT_r], writes=[red_r])
                        t1, t1_r = tm[2]
                        P.op(dve, lambda: V.tensor_tensor(out=t1[:, :N], in0=red[:, ch, :], in1=zB[:, ch, 15:15 + N], op=ALU.add),
                             reads=[red_r] + zB_r, writes=[t1_r])
                        P.op(dve, lambda: V.scalar_tensor_tensor(out=d_[:, :N], in0=t1[:, :N], scalar=invw[:, ch:ch + 1], in1=zB[:, ch, 15:15 + N],
                                                                 op0=ALU.mult, op1=ALU.subtract), reads=[t1_r, invw_r] + zB_r, writes=[d_r])
                    small_out(nbs[l, :, 14, :], [(zB[:, ch, 15:15 + N], zB_r) for ch in range(2)], NS)

            def mixB2():
                bd_, bd_r = BD[l]
                for ch in range(2):
                    d_, d_r = dB[ch]
                    bk, bk_r = nb()
                    mm_group(bk[:, :N], bk_r, [(bd_[:, ch, :], d_[:, :N])], reads=[bd_r, d_r])
                    P.op(act, lambda: S.activation(out=yc[:, 2 + ch, :N], in_=bk[:, :N], func=AF.Identity, scale=pcol(C_PSC + ch)),
                         reads=[bk_r, pT_r], writes=[yc_r[2 + ch]])


            def mixC1():
                g_, g_r = gcb[l]
                ntb = 1 if sample else 4
                mt = NS if sample else 128
                for tb in range(ntb):
                    P.op(dve, lambda: V.bn_stats(out=bst[0:mt, tb, :], in_=cv[0:mt, tb, :]), reads=cv_r, writes=[bst_r])
                    P.op(dve, lambda: V.bn_aggr(out=bmv[0:mt, tb, :], in_=bst[0:mt, tb, :]), reads=[bst_r], writes=[bmv_r])
                P.op(act, lambda: S.activation(out=brs[0:mt, 0:ntb], in_=bmv[0:mt, 0:ntb, 1], func=AF.Sqrt, scale=1.0, bias=LN_EPS), reads=[bmv_r], writes=[brs_r])
                P.op(dve, lambda: V.reciprocal(out=brs[0:mt, 0:ntb], in_=brs[0:mt, 0:ntb]), reads=[brs_r], writes=[brs_r])
                for tb in range(ntb):
                    P.op(dve, lambda: V.tensor_scalar(out=cv[0:mt, tb, :], in0=cv[0:mt, tb, :], scalar1=bmv[0:mt, tb, 0:1], scalar2=brs[0:mt, tb:tb + 1],
                                                      op0=ALU.subtract, op1=ALU.mult), reads=cv_r + [bmv_r, brs_r], writes=cv_r)
                P.op(pool, lambda: G.tensor_tensor(out=cv[0:mt, 0:ntb, :], in0=cv[0:mt, 0:ntb, :], in1=g_[0:mt, 0, :].unsqueeze(1).broadcast_to([mt, ntb, 256]), op=ALU.mult),
                     reads=cv_r + [g_r], writes=cv_r)
                P.op(pool, lambda: G.tensor_tensor(out=cv[0:mt, 0:ntb, :], in0=cv[0:mt, 0:ntb, :], in1=g_[0:mt, 1, :].unsqueeze(1).broadcast_to([mt, ntb, 256]), op=ALU.add),
                     reads=cv_r + [g_r], writes=cv_r)

            def mixC2():
                b_, b_r = bsb[l]
                t1, t1_r = tm[2]
                if not sample:
                    wm_, wm_r = wmT[l]
                    for h in range(4):
                        ch = h // 2
                        rows = slice(64 * (h % 2), 64 * (h % 2) + 64)
                        bk, bk_r = nb()
                        for tb in range(4):
                            P.op(pe, lambda: T.matmul(bk[:, tb * 128:(tb + 1) * 128], cv[:, tb, ch * 128:(ch + 1) * 128], wm_[:, h, :], start=True, stop=True),
                                 reads=cv_r + [wm_r], writes=[bk_r], inc=(tb == 3))
                        P.op(dve, lambda: V.tensor_tensor(out=t1[rows, 0:NT].rearrange("p (a b) -> p a b", a=4), in0=bk[rows, :].rearrange("p (a b) -> p a b", a=4),
                                                          in1=b_[rows, ch, :].unsqueeze(1).broadcast_to([64, 4, 128]), op=ALU.add),
                             reads=[bk_r, b_r], writes=[t1_r])
                        P.op(pool, lambda: G.tensor_tensor(out=yc[rows, 4 + ch, :N], in0=t1[rows, :N], in1=big[rows, 6 + ch, :N], op=ALU.mult),
                             reads=[t1_r, big_r[6 + ch]], writes=[yc_r[4 + ch]])
                else:
                    P.dma(pool, P.dsem("nvs%d" % l), [(nvs[l], cv[0:NS, 0, :])], reads=cv_r, is_output=True)
                    w0_, w0_r = w00[l]
                    bk, bk_r = nb()
                    for ch in range(2):
                        P.op(pe, lambda: T.transpose(bk[:, ch * NS:(ch + 1) * NS], cv[0:NS, 0, ch * 128:(ch + 1) * 128], ident[0:NS, 0:NS]),
                             reads=cv_r + [ident_r], writes=[bk_r], inc=(ch == 1))
                    for ch in range(2):
                        P.op(dve, lambda: V.tensor_scalar(out=t1[:, :N], in0=bk[:, ch * NS:(ch + 1) * NS], scalar1=w0_[:, ch:ch + 1], scalar2=b_[:, ch, 0:1],
                                                          op0=ALU.mult, op1=ALU.add), reads=[bk_r, w0_r, b_r], writes=[t1_r])
                        P.op(dve, lambda: V.tensor_tensor(out=yc[:, 4 + ch, :N], in0=t1[:, :N], in1=big[:, 6 + ch, :N], op=ALU.mult),
                             reads=[t1_r, big_r[6 + ch]], writes=[yc_r[4 + ch]])


            def mixD():
                hD, hD_r = haloD[l]
                for ch in range(2):
                    P.op(pool, lambda: G.tensor_tensor(out=zD[:, ch, 2:2 + N], in0=big[:, 12 + ch, :N], in1=big[:, 14 + ch, :N], op=ALU.mult),
                         reads=[big_r[12 + ch], big_r[14 + ch]], writes=zD_r)
                if not sample:
                    if first:
                        P.op(pool, lambda: G.memset(zD[:, :, 0:2], 0.0), writes=zD_r)
                    else:
                        P.op(pool, lambda: G.tensor_copy(out=zD[:, :, 0:2], in_=hD[:]), reads=[hD_r], writes=zD_r)
                for ch in range(2):
                    a_, a_r = acc[2 * ch + 1]
                    if not sample:
                        ins = [zD[:, ch, k:k + N] for k in range(3)]
                        rr = list(zD_r)
                    else:
                        sv = stDT[:, ch, :].rearrange("p (s k) -> p s k", s=NS)
                        ins = [sv[:, :, 0], sv[:, :, 1], zD[:, ch, 2:2 + N]]
                        rr = zD_r + [stDT_r]
                    P.op(dve, lambda: V.tensor_scalar(out=a_[:, :N], in0=ins[0], scalar1=pcol(C_CDW + 3 * ch), scalar2=None, op0=ALU.mult),
                         reads=rr + [pT_r], writes=[a_r])
                    for k in (1, 2):
                        P.op(dve, lambda: V.scalar_tensor_tensor(out=a_[:, :N], in0=ins[k], scalar=pcol(C_CDW + 3 * ch + k), in1=a_[:, :N], op0=ALU.mult, op1=ALU.add),
                             reads=rr + [pT_r, a_r], writes=[a_r])
                    P.op(pool, lambda: G.tensor_tensor(out=yc[:, 6 + ch, :N], in0=a_[:, :N], in1=big[:, 10 + ch, :N], op=ALU.mult),
                         reads=[a_r, big_r[10 + ch]], writes=[yc_r[6 + ch]])
                if not sample:
                    if last:
                        small_out(ndp[l], [(zD[:, ch, N:N + 2], zD_r) for ch in range(2)], 2)
                    else:
                        P.op(pool, lambda: G.tensor_copy(out=hD[:], in_=zD[:, :, N:N + 2]), reads=zD_r, writes=[hD_r])
                else:
                    small_out(nds[l, :, 1, :], [(zD[:, ch, 2:2 + N], zD_r) for ch in range(2)], NS)


            gA = mixA1()

            def stepA():
                for _ in gA:
                    return
            def inproj_m(a):
                inproj(a)
                if ti == 0 and l == 0:
                    mod_load(0, 4 + a)
                    if a == 1:
                        mod_finish(0, ("G1",))
                    if a == 7:
                        mod_finish(0, ("A2", "G2"))
            inproj_m(0); inproj_m(1); stepA(); inproj_m(2); stepA(); inproj_m(3); stepA(); inproj_m(4); stepA()
            for _ in gA:
                pass
            def ycat_split(c):
                if sample:
                    return
                P.op(act, lambda: S.copy(out=hlo_t[:, c, :N], in_=yc[:, c, :N]), reads=[yc_r[c]], writes=[hT_r[c]])
                P.op(dve, lambda: V.tensor_tensor(out=hhi_t[:, c, :N], in0=yc[:, c, :N], in1=hlo_t[:, c, :N].bitcast(F32), op=ALU.subtract),
                     reads=[yc_r[c], hT_r[c]], writes=[yc_r[c]])
            inproj_m(5); mixB1(); inproj_m(6); mixA2a(); mixC1(); inproj_m(7)
            mixA2b(); mixD(); ycat_split(6); ycat_split(7); mixB2(); ycat_split(2); ycat_split(3); mixC2(); ycat_split(4); ycat_split(5)
            ycat_split(0); ycat_split(1)

            korder = [6, 7, 2, 3, 4, 5, 0, 1]
            st = StatAcc(N)
            for a in range(4):
                s, s_r = wnext("out", l, a)
                sp_ = None if sample else (whi(s), wlo(s), hlo_t[:], hhi_t[:], yc_r + hT_r)
                for j, ps, ps_r in proj_blocks(wraw(s), s_r, range(2), yc, yc_r, 8, N, sample, split=sp_, kc_major=(a == 0), kc_order=korder):
                    ob = 2 * a + j
                    if ob % 2 == 0 and not sample:
                        P.op(dve, lambda: V.tensor_copy(big[:, ob, :N], ps), reads=[ps_r], writes=[big_r[ob]])
                    else:
                        P.op(act, lambda: S.copy(out=big[:, ob, :N], in_=ps), reads=[ps_r], writes=[big_r[ob]])
                    if not sample:
                        st.square(big[:, ob, :N], big_r[ob], "pool")
            if sample:
                sample_stats(big[:, 0:8, 0:NS], big_r[0:8], RMS_EPS, D)
            else:
                st.finish(RMS_EPS, D)
            resid_update(l, big, big_r, G1[l], N, sample, want_stats=True)

            def hact_split(kc):
                hi_ap, hi_r = HACT_HI[kc]
                P.op(act, lambda: S.copy(out=hi_ap[:, :N], in_=big[:, kc, :N]), reads=[big_r[kc]], writes=[hi_r])
                P.op(dve, lambda: V.tensor_tensor(out=hactlo_t[:, kc, :N], in0=big[:, kc, :N], in1=hi_ap[:, :N].bitcast(F32), op=ALU.subtract),
                     reads=[big_r[kc], hi_r], writes=[big_r[kc]])

            modulate(l, A2[l], 24, N, sample)
            for cg in range(11):
                sg, sg_r = wnext("gate", l, cg)
                sp_ = None if sample else (whi(sg), wlo(sg), h_hi, h_lo, hT_r + yc_r)
                for j, ps, ps_r in proj_blocks(wraw(sg), sg_r, range(2), hT, hT_r, 8, N, sample, split=sp_, kc_major=(cg == 0)):
                    fb = 2 * cg + j
                    P.op(act, lambda: S.activation(out=big[:, fb, :N], in_=ps, func=AF.Silu), reads=[ps_r], writes=[big_r[fb]])
                su, su_r = wnext("up", l, cg)
                sp_ = None if sample else (whi(su), wlo(su), h_hi, h_lo, hT_r + yc_r)
                for j, ps, ps_r in proj_blocks(wraw(su), su_r, range(2), hT, hT_r, 8, N, sample, split=sp_):
                    fb = 2 * cg + j
                    P.op(dve, lambda: V.tensor_tensor(out=big[:, fb, :N], in0=big[:, fb, :N], in1=ps, op=ALU.mult), reads=[big_r[fb], ps_r], writes=[big_r[fb]])
                    if not sample and fb < 6:
                        hact_split(fb)
                if ti == 0 and l == 0:
                    mod_load(1, cg)
            if ti == 0 and l == 0:
                mod_load(1, 11)
                mod_finish(1)
            st = StatAcc(N)
            tokb = None
            if not sample:
                for kc in range(6, NFF):
                    hact_split(kc)
                f_aps, f_regs = [], []
                bank0 = None
                for ob in range(8):
                    bk, bk_r = nb()
                    if ob == 0:
                        bank0 = (bk, bk_r)
                    for hf in range(2):
                        s, s_r = wnext("downh", l, 2 * ob + hf)
                        w_h = whi_t[s][:, 0:1408].rearrange("p (k c) -> p k c", k=11)
                        w_l = wlo_t[s][:, 0:1408].rearrange("p (k c) -> p k c", k=11)
                        pairs, rr = [], [s_r]
                        for k in range(11):
                            kc = 11 * hf + k
                            hi_ap, hi_r = HACT_HI[kc]
                            pairs += [(w_h[:, k, :], hi_ap[:, :N]), (w_h[:, k, :], hactlo_t[:, kc, :N]), (w_l[:, k, :], hi_ap[:, :N])]
                            rr += [hi_r, big_r[kc]]
                        n = len(pairs)
                        for i, (l_ap, r_ap) in enumerate(pairs):
                            P.op(pe, (lambda l_ap=l_ap, r_ap=r_ap, i=i: T.matmul(bk[:, :N], l_ap, r_ap, start=(hf == 0 and i == 0), stop=(hf == 1 and i == n - 1))),
                                 reads=rr, writes=[bk_r], inc=(i == n - 1))
                    if ob == 0:
                        t2, t2_r = tm[2]
                        P.op(dve, lambda: V.tensor_copy(t2[:, :N], bk[:, :N]), reads=[bk_r], writes=[t2_r])
                        f_aps.append(t2[:, :N]); f_regs.append(t2_r)
                        st.square(t2[:, :N], t2_r, "pool")
                    else:
                        f_aps.append(bk[:, :N]); f_regs.append(bk_r)
                        st.square(bk[:, :N], bk_r, "act")
                pbi[0] = pb.index(bank0[0])
                st.finish(RMS_EPS, D)
                resid_update(l, None, f_regs, G2[l], N, sample, want_stats=(l == 0), src_aps=f_aps, stat_bank=bank0)
                return
            for ob in range(8):
                s, s_r = wnext("down", l, ob)
                wv = wview22(s)
                if not sample:
                    bk, bk_r = nb()
                    mm_group(bk[:, :N], bk_r, [(wv[:, kc, :], big[:, kc, :N]) for kc in range(NFF)], reads=[s_r] + big_r)
                    pss = [(ob, bk[:, :N], bk_r)]
                else:
                    if ob % 4 == 0:
                        tokb = nb()
                    bk, bk_r = tokb
                    q = ob % 4
                    mm_group(bk[0:NS, q * 128:(q + 1) * 128], bk_r, [(big[:, kc, 0:NS], wv[:, kc, :]) for kc in range(NFF)], reads=[s_r] + big_r)
                    pss = []
                    if q == 3:
                        tk, tk_r = tm[tki[0] % 2]
                        tki[0] += 1
                        P.op(act, lambda: S.copy(out=tk[0:NS, 0:512], in_=bk[0:NS, 0:512]), reads=[bk_r], writes=[tk_r])
                        bk2, bk2_r = nb()
                        for qq in range(4):
                            transpose(bk2[:, qq * NS:(qq + 1) * NS], bk2_r, tk[0:NS, qq * 128:(qq + 1) * 128], [tk_r], NS, inc=(qq == 3))
                        pss = [(ob - 3 + qq, bk2[:, qq * NS:(qq + 1) * NS], bk2_r) for qq in range(4)]
                for (o2, ps, ps_r) in pss:
                    if o2 % 2 == 0 and not sample:
                        P.op(dve, lambda: V.tensor_copy(yc[:, o2, :N], ps), reads=[ps_r], writes=[yc_r[o2]])
                    else:
                        P.op(act, lambda: S.copy(out=yc[:, o2, :N], in_=ps), reads=[ps_r], writes=[yc_r[o2]])
            sample_stats(yc[:, :, 0:NS], yc_r, RMS_EPS, D)
            if DBG and l == 0 and ti == 0:
                P.dma(pool, P.dsem("dbg4"), [(dbg_h2, hT[:]), (dbg_f, yc[:]), (dbg_hact, big[:])], reads=yc_r + hT_r + big_r, is_output=True)
            resid_update(l, yc, yc_r, G2[l], N, sample, want_stats=(l == 0))
            if DBG and l == 0 and ti == 0:
                P.dma(pool, P.dsem("dbg3"), [(dbg_x2, xT[:])], reads=xT_r, is_output=True)

        for ti in range(NTILES + 1):
            sample = (ti == NTILES)
            N = NS if sample else NT
            if not sample:
                banks = [nb() for _ in range(8)]
                for tb in range(4):
                    s_, s_r = stg[tb % 2]
                    r0 = ti * NT + tb * 128
                    P.dma(pool, stg_d[tb % 2], [(s_[:, :], xp[r0:r0 + 128, :])], writes=[s_r])
                    for dc in range(8):
                        bk, bk_r = banks[dc]
                        transpose(bk[:, tb * 128:(tb + 1) * 128], bk_r, s_[:, dc * 128:(dc + 1) * 128], [s_r], 128, inc=(dc == 7 or tb == 3))
                for dc in range(8):
                    bk, bk_r = banks[dc]
                    if dc % 2 == 0:
                        P.op(dve, lambda: V.tensor_copy(xT[:, dc, :], bk[:, :]), reads=[bk_r], writes=[xT_r[dc]])
                    else:
                        P.op(act, lambda: S.copy(out=xT[:, dc, :], in_=bk[:, :]), reads=[bk_r], writes=[xT_r[dc]])
            else:
                s_, s_r = stg[0]
                P.dma(pool, stg_d[0], [(s_[0:NS, :], xs)], writes=[s_r])
                bk, bk_r = nb()
                for dc in range(8):
                    transpose(bk[:, dc * NS:(dc + 1) * NS], bk_r, s_[0:NS, dc * 128:(dc + 1) * 128], [s_r], NS, inc=(dc == 7))
                P.op(dve, lambda: V.tensor_copy(xT[:, :, 0:NS], bk[:, 0:8 * NS].rearrange("p (a b) -> p a b", a=8)), reads=[bk_r], writes=xT_r)

            for l in range(2):
                if sample:
                    P.dma(pool, d_c2c, [(nas[l, :, 0:29, :], sa[l, :, 1:30, :]), (nbs[l, :, 0:14, :], sbb[l, :, 1:15, :]),
                                        (nds[l, :, 0:1, :], sd[l, :, 1:2, :])], is_output=True)
                    for (src, T_, T_r, K) in ((sa, stAT, stAT_r, 30), (sbb, stBT, stBT_r, 15), (sd, stDT, stDT_r, 2)):
                        flat = src[l].rearrange("s k c -> (s k) c")
                        nrows = NS * K
                        r = 0
                        gi = 0
                        while r < nrows:
                            n = min(120, nrows - r)
                            s_, s_r = stg[gi % 2]
                            P.dma(pool, stg_d[gi % 2], [(s_[0:n, 0:256], flat[r:r + n, :])], writes=[s_r])
                            bk, bk_r = nb()
                            for ch in range(2):
                                transpose(bk[:, ch * 128:ch * 128 + n], bk_r, s_[0:n, ch * 128:(ch + 1) * 128], [s_r], n, inc=(ch == 1))
                            P.op(dve, lambda: V.tensor_copy(T_[:, :, r:r + n], bk[:, 0:256].rearrange("p (a b) -> p a b", a=2)[:, :, 0:n]),
                                 reads=[bk_r], writes=[T_r])
                            r += n
                            gi += 1
                block(l, ti, sample, have_stats=(l == 1))

            if not sample:
                for tb in range(4):
                    o_, o_r = ost[tb % 2]
                    for half in range(2):
                        bk, bk_r = nb()
                        for q in range(4):
                            dc = 4 * half + q
                            transpose(bk[:, q * 128:(q + 1) * 128], bk_r, xT[:, dc, tb * 128:(tb + 1) * 128], [xT_r[dc]], 128, inc=(q == 3))
                        if half == 0:
                            P.op(dve, lambda: V.tensor_copy(o_[:, 0:512], bk[:, :]), reads=[bk_r], writes=[o_r])
                        else:
                            P.op(act, lambda: S.copy(out=o_[:, 512:1024], in_=bk[:, :]), reads=[bk_r], writes=[o_r])
                    r0 = ti * NT + tb * 128
                    P.dma(pool, ost_d[tb % 2], [(yp[r0:r0 + 128, :], o_[:, :])], reads=[o_r], is_output=True)
            else:
                o_, o_r = ost[0]
                for half in range(2):
                    bk, bk_r = nb()
                    for q in range(4):
                        dc = 4 * half + q
                        transpose(bk[0:NS, q * 128:(q + 1) * 128], bk_r, xT[:, dc, 0:NS], [xT_r[dc]], 128, inc=(q == 3))
                    P.op(dve, lambda: V.tensor_copy(o_[0:NS, 512 * half:512 * half + 512], bk[0:NS, :]), reads=[bk_r], writes=[o_r])
                P.dma(pool, ost_d[0], [(ys, o_[0:NS, :])], reads=[o_r], is_output=True)

        assert wpos[0] == len(wseq), (wpos[0], len(wseq))
        P.finish()
        build_program.stats = (P.n_inst, P.n_sem, {e.name: e.cnt for e in P.engs})
    return nc


def _consts():
    ident = np.eye(128, dtype=np.float32)
    tril = np.tril(np.ones((128, 128), dtype=np.float32))
    invw = np.zeros((128, 2), np.float32)
    invcnt = np.zeros((128, 2, 16), np.float32)
    for ch in range(2):
        for half in range(2):
            w = POOLW[2 * ch + half]
            invw[64 * half:64 * half + 64, ch] = 1.0 / w
            for n in range(16):
                invcnt[64 * half:64 * half + 64, ch, n] = 1.0 / min(w, n + 1)
    return ident, tril, invw, invcnt


def kernel(x_prompt, x_sample, c_prompt, c_sample, state_conv_a, state_pool_b, state_conv_d,
           w_ada, b_ada, g_pre_mix, g_post_mix, w_in, conv_a_w, conv_a_b, ln_a_g, ln_a_b,
           pool_w, pool_scale, ln_c_g, ln_c_b, sgu_w, sgu_b, conv_d_w, w_out,
           g_pre_ffn, g_post_ffn, w_gate, w_up, w_down):
    f = lambda a: np.ascontiguousarray(np.asarray(a, dtype=np.float32))
    x_prompt, x_sample, c_prompt, c_sample = f(x_prompt), f(x_sample), f(c_prompt), f(c_sample)
    state_conv_a, state_pool_b, state_conv_d = f(state_conv_a), f(state_pool_b), f(state_conv_d)
    shared = dict(w_ada=f(w_ada), b_ada=f(b_ada), g_pre_mix=f(g_pre_mix), g_post_mix=f(g_post_mix), w_in=f(w_in),
                  conv_a_w=f(conv_a_w), conv_a_b=f(conv_a_b), ln_a_g=f(ln_a_g), ln_a_b=f(ln_a_b), pool_w=f(pool_w),
                  pool_scale=f(pool_scale), ln_c_g=f(ln_c_g), ln_c_b=f(ln_c_b), sgu_w=f(sgu_w), sgu_b=f(sgu_b),
                  conv_d_w=f(conv_d_w), w_out=f(w_out), g_pre_ffn=f(g_pre_ffn), g_post_ffn=f(g_post_ffn),
                  w_gate=f(w_gate), w_up=f(w_up), w_down=f(w_down))
    ident, tril, invw, invcnt = _consts()
    shared.update(c_ident=ident, c_tril=tril, c_invw=invw, c_invcnt=invcnt)
    n = 8
    in_maps = []
    for i in range(n):
        sl = slice(NS * i, NS * i + NS)
        m = dict(shared)
        m["xp"] = x_prompt[i]
        m["xs"] = np.ascontiguousarray(x_sample[sl, 0, :])
        m["cc"] = np.ascontiguousarray(np.concatenate([c_prompt[i:i + 1], c_sample[sl]], axis=0))
        m["sa"] = np.ascontiguousarray(state_conv_a[:, sl])
        m["sbb"] = np.ascontiguousarray(state_pool_b[:, sl])
        m["sd"] = np.ascontiguousarray(state_conv_d[:, sl])
        in_maps.append(m)
    nc = build_program()
    res = run_bass_kernel_spmd(nc, in_maps, core_ids=list(range(n)))
    R = res.results
    y_prompt = np.stack([R[i]["yp"] for i in range(n)], axis=0)
    y_sample = np.concatenate([R[i]["ys"] for i in range(n)], axis=0)[:, None, :]
    na_p = np.stack([R[i]["nap"] for i in range(n)], axis=1)
    nb_p = np.stack([R[i]["nbp"] for i in range(n)], axis=1)
    nd_p = np.stack([R[i]["ndp"] for i in range(n)], axis=1)
    na_s = np.concatenate([R[i]["nas"] for i in range(n)], axis=1)
    nb_s = np.concatenate([R[i]["nbs"] for i in range(n)], axis=1)
    nd_s = np.concatenate([R[i]["nds"] for i in range(n)], axis=1)
    nv_s = np.concatenate([R[i]["nvs"] for i in range(n)], axis=1)[:, :, None, :]
    outs = (y_prompt, y_sample, na_p, nb_p, nd_p, na_s, nb_s, nd_s, nv_s)
    return tuple(np.ascontiguousarray(o, dtype=np.float32) for o in outs)
```

```python
import contextlib
import numpy as np
import concourse.bass as bass
import concourse.mybir as mybir
from concourse.bass_utils import run_bass_kernel_spmd

F32 = mybir.dt.float32
F32R = mybir.dt.float32r
AF = mybir.ActivationFunctionType
ALU = mybir.AluOpType
AX = mybir.AxisListType

D = 1024
SEQ = 2048
NT = 512
NTILES = SEQ // NT
NS = 16
DFF = 2816
NFF = DFF // 128
RMS_EPS = 1e-6
LN_EPS = 1e-5
POOLW = (2, 4, 8, 16)

C_BADA, C_GPM, C_GOM, C_GPF, C_GOF, C_CAB, C_LAG, C_LAB, C_PSC, C_CDW = 0, 48, 56, 64, 72, 80, 82, 84, 86, 88
NS1 = 94
C_CAW = 94
NPAR = 94 + 62


class Reg:
    __slots__ = ("w", "rs", "name", "psum")

    def __init__(self, name="", psum=False):
        self.w = None
        self.rs = {}
        self.name = name
        self.psum = psum


class Eng:
    def __init__(self, name, handle, sem):
        self.name = name
        self.h = handle
        self.sem = sem
        self.cnt = 0
        self.waited = {}


class DSem:
    def __init__(self, sem):
        self.sem = sem
        self.cnt = 0


class Prog:
    def __init__(self, nc, es):
        self.nc = nc
        self.es = es
        self.n_sem = 0
        self.pe = self._eng("pe", nc.tensor)
        self.dve = self._eng("dve", nc.vector)
        self.act = self._eng("act", nc.scalar)
        self.pool = self._eng("pool", nc.gpsimd)
        self.sp = self._eng("sp", nc.sync)
        self.engs = [self.pe, self.dve, self.act, self.pool, self.sp]
        self.out_tokens = []
        self.n_inst = 0

    def new_sem(self, name):
        self.n_sem += 1
        return self.es.enter_context(self.nc.semaphore(name))

    def _eng(self, name, h):
        return Eng(name, h, self.new_sem("prog_" + name))

    def dsem(self, name):
        return DSem(self.new_sem("d_" + name))

    def sb(self, name, shape, dtype=F32):
        return self.es.enter_context(self.nc.sbuf_tensor(name, list(shape), dtype))

    def ps(self, name, shape, dtype=F32):
        return self.es.enter_context(self.nc.psum_tensor(name, list(shape), dtype))

    def _wait(self, eng, reads, writes):
        need = {}

        def add(tok, raw):
            if tok is None:
                return
            sem, val, src = tok
            if src is eng and not raw and eng.name == "pe":
                return
            k = id(sem)
            if k not in need or need[k][1] < val:
                need[k] = (sem, val)

        for r in reads:
            add(r.w, True)
            if r.psum:
                for tok in r.rs.values():
                    if tok[2] is not eng:
                        add(tok, False)
        for w in writes:
            add(w.w, False)
            for tok in w.rs.values():
                add(tok, False)
        for k, (sem, val) in need.items():
            if eng.waited.get(k, 0) < val:
                eng.h.wait_ge(sem, val)
                eng.waited[k] = val

    def _commit(self, tok, reads, writes):
        k = id(tok[0])
        for r in reads:
            old = r.rs.get(k)
            if old is None or old[1] < tok[1]:
                r.rs[k] = tok
        for w in writes:
            w.w = tok
            w.rs = {}

    def op(self, eng, fn, reads=(), writes=(), inc=True):
        self._wait(eng, reads, writes)
        inst = fn()
        self.n_inst += 1
        if inc:
            eng.cnt += 1
            inst.then_inc(eng.sem, 1)
            tok = (eng.sem, eng.cnt, eng)
        else:
            tok = (eng.sem, eng.cnt + 1, eng)
        self._commit(tok, reads, writes)
        return tok

    def dma(self, q, ds, pairs, reads=(), writes=(), is_output=False):
        self._wait(q, reads, writes)
        for (o, i) in pairs:
            q.h.dma_start(out=o, in_=i).then_inc(ds.sem, 16)
            ds.cnt += 16
            self.n_inst += 1
        tok = (ds.sem, ds.cnt, None)
        self._commit(tok, reads, writes)
        if is_output:
            self.out_tokens.append(tok)
        return tok

    def finish(self):
        need = {}
        for (sem, val, _) in self.out_tokens:
            k = id(sem)
            if k not in need or need[k][1] < val:
                need[k] = (sem, val)
        for e in self.engs:
            if e is not self.sp and e.cnt > 0:
                need[id(e.sem)] = (e.sem, e.cnt)
        for k, (sem, val) in need.items():
            self.sp.h.wait_ge(sem, val)


def build_program():
    nc = bass.Bass("TRN2", target_bir_lowering=False)

    def din(name, shape):
        return nc.dram_tensor(name, list(shape), F32, kind="ExternalInput").ap()

    def dout(name, shape):
        return nc.dram_tensor(name, list(shape), F32, kind="ExternalOutput").ap()

    xp = din("xp", [SEQ, D]); xs = din("xs", [NS, D]); cc = din("cc", [NS + 1, D])
    sa = din("sa", [2, NS, 30, 256]); sbb = din("sbb", [2, NS, 15, 256]); sd = din("sd", [2, NS, 2, 256])
    w_ada = din("w_ada", [2, D, 6 * D]); b_ada = din("b_ada", [2, 6 * D])
    g_pre_mix = din("g_pre_mix", [2, D]); g_post_mix = din("g_post_mix", [2, D])
    w_in = din("w_in", [2, D, 2048]); conv_a_w = din("conv_a_w", [2, 31, 256]); conv_a_b = din("conv_a_b", [2, 256])
    ln_a_g = din("ln_a_g", [2, 256]); ln_a_b = din("ln_a_b", [2, 256])
    pool_w = din("pool_w", [2, 4, 64, 64]); pool_scale = din("pool_scale", [2, 256])
    ln_c_g = din("ln_c_g", [2, 256]); ln_c_b = din("ln_c_b", [2, 256])
    sgu_w = din("sgu_w", [2, 4, 128, 128]); sgu_b = din("sgu_b", [2, 4, 128])
    conv_d_w = din("conv_d_w", [2, 3, 256]); w_out = din("w_out", [2, D, D])
    g_pre_ffn = din("g_pre_ffn", [2, D]); g_post_ffn = din("g_post_ffn", [2, D])
    w_gate = din("w_gate", [2, D, DFF]); w_up = din("w_up", [2, D, DFF]); w_down = din("w_down", [2, DFF, D])
    c_ident = din("c_ident", [128, 128]); c_tril = din("c_tril", [128, 128])
    c_invw = din("c_invw", [128, 2]); c_invcnt = din("c_invcnt", [128, 2, 16])

    yp = dout("yp", [SEQ, D]); ys = dout("ys", [NS, D])
    nap = dout("nap", [2, 30, 256]); nbp = dout("nbp", [2, 15, 256]); ndp = dout("ndp", [2, 2, 256])
    nas = dout("nas", [2, NS, 30, 256]); nbs = dout("nbs", [2, NS, 15, 256]); nds = dout("nds", [2, NS, 2, 256])
    nvs = dout("nvs", [2, NS, 256])

    import os as _os
    DBG = _os.environ.get("KDBG") == "1"
    if DBG:
        dbg_yc = dout("dbg_yc", [128, 8, NT]); dbg_x1 = dout("dbg_x1", [128, 8, NT]); dbg_m = dout("dbg_m", [128, 8, NT])
        dbg_x2 = dout("dbg_x2", [128, 8, NT]); dbg_h2 = dout("dbg_h2", [128, 8, NT]); dbg_f = dout("dbg_f", [128, 8, NT])
        dbg_hact = dout("dbg_hact", [128, NFF, NT])
    es = contextlib.ExitStack()
    with es:
        P = Prog(nc, es)
        pe, dve, act, pool, sp = P.pe, P.dve, P.act, P.pool, P.sp
        V, S, G, T = nc.vector, nc.scalar, nc.gpsimd, nc.tensor

        xT = P.sb("xT", [128, 8, NT]); xT_r = [Reg("xT%d" % i) for i in range(8)]
        hT = P.sb("hT", [128, 8, NT]); hT_r = [Reg("hT%d" % i) for i in range(8)]
        yc = P.sb("yc", [128, 8, NT]); yc_r = [Reg("yc%d" % i) for i in range(8)]
        big = P.sb("big", [128, NFF, NT]); big_r = [Reg("big%d" % i) for i in range(NFF)]
        NSLOT = 3
        wsl = [P.sb("wsl%d" % i, [128, 4096]) for i in range(NSLOT)]
        wsl_r = [Reg("wsl%d" % i) for i in range(NSLOT)]
        wsl_d = [P.dsem("wsl%d" % i) for i in range(NSLOT)]
        pb = [P.ps("pb%d" % i, [128, 512]) for i in range(8)]
        pb_r = [Reg("pb%d" % i, psum=True) for i in range(8)]
        pbi = [0]

        held = set()

        def nb(hold=False):
            while True:
                k = pbi[0] % 8
                pbi[0] += 1
                if k not in held:
                    break
            if hold:
                held.add(k)
            return pb[k], pb_r[k]

        def tile(name, shape):
            return P.sb(name, shape), Reg(name)

        ident, ident_r = tile("ident", [128, 128])
        tril, tril_r = tile("tril", [128, 128])
        ones, ones_r = tile("ones", [128, 128])
        invw, invw_r = tile("invw", [128, 2])
        invcnt, invcnt_r = tile("invcnt", [128, 2, 16])
        stg = [tile("stg%d" % i, [128, 1024]) for i in range(2)]
        stg_d = [P.dsem("stg%d" % i) for i in range(2)]
        ost = stg
        ost_d = stg_d
        parT = [tile("parT%d" % l, [128, NPAR]) for l in range(2)]
        gcb = [tile("gcb%d" % l, [128, 2, 256]) for l in range(2)]
        bsb = [tile("bsb%d" % l, [128, 2, 128]) for l in range(2)]
        w00 = [tile("w00%d" % l, [128, 2]) for l in range(2)]
        wmT = [tile("wmT%d" % l, [128, 4, 128]) for l in range(2)]
        BD = [tile("BD%d" % l, [128, 2, 128]) for l in range(2)]
        modT = [tile("modT%d" % l, [128, 48, NS + 1]) for l in range(2)]
        A1 = [tile("A1_%d" % l, [128, 8, NS + 1]) for l in range(2)]
        G1 = [tile("G1_%d" % l, [128, 8, NS + 1]) for l in range(2)]
        A2 = [tile("A2_%d" % l, [128, 8, NS + 1]) for l in range(2)]
        G2 = [tile("G2_%d" % l, [128, 8, NS + 1]) for l in range(2)]
        scT, scT_r = tile("scT", [128, 8, NS + 1])
        haloA = [tile("haloA%d" % l, [128, 2, 30]) for l in range(2)]
        haloB = [tile("haloB%d" % l, [128, 2, 15]) for l in range(2)]
        haloD = [tile("haloD%d" % l, [128, 2, 2]) for l in range(2)]
        zA = big[:, 16:19, :].rearrange("p a b -> p (a b)")[:, 0:2 * (30 + NT)].rearrange("p (c n) -> p c n", c=2)
        zA_r = big_r[16:19]
        zB = big[:, 19:22, :].rearrange("p a b -> p (a b)")[:, 0:2 * (15 + NT)].rearrange("p (c n) -> p c n", c=2)
        zB_r = big_r[19:22]
        zD, zD_r0 = tile("zD", [128, 2, 2 + NT])
        zD_r = [zD_r0]
        acc = [tile("acc%d" % i, [128, 15 + NT]) for i in range(4)]
        sq = [tile("sq%d" % i, [128, 15 + NT]) for i in range(2)]
        rstd, rstd_r = tile("rstd", [128, NT])
        ssum, ssum_r = tile("ssum", [128, NT])
        tm = [tile("tm%d" % i, [128, 15 + NT]) for i in range(3)]
        PQ = [tm[0], sq[0], tm[1], sq[1]]
        cv = big[:, 8:10, :].rearrange("p a (t c) -> p (a t) c", c=256)
        cv_r = big_r[8:10]
        bst, bst_r = tile("bst", [128, 4, 6])
        bmv, bmv_r = tile("bmv", [128, 4, 2])
        brs, brs_r = tile("brs", [128, 4])
        stAT, stAT_r = tile("stAT", [128, 2, NS * 30])
        stBT, stBT_r = tile("stBT", [128, 2, NS * 15])
        stDT, stDT_r = tile("stDT", [128, 2, NS * 2])
        prd, prd_r = tm[2][0][:, 0:NS * 30], tm[2][1]
        red, red_r = tile("red", [128, 2, NS])
        sm = stg
        sm_d = stg_d
        smi = [0]
        d_misc = P.dsem("misc")
        d_c2c = P.dsem("c2c")

        def wview8(i, cols):
            return wsl[i][:, 0:8 * cols].rearrange("p (k c) -> p k c", k=8)

        def wview22(i):
            return wsl[i][:, 0:NFF * 128].rearrange("p (k c) -> p k c", k=NFF)

        def wraw(i):
            return wsl[i][:, 0:2048].rearrange("p (k c) -> p k c", k=8)

        def _addr(t):
            return nc.lookup_mloc(t.name).addr

        wlo_t = [nc.alloc_sbuf_tensor_at("wlo%d" % i, [128, 2048], F32R, offset=_addr(wsl[i])) for i in range(NSLOT)]
        whi_t = [nc.alloc_sbuf_tensor_at("whi%d" % i, [128, 2048], F32R, offset=_addr(wsl[i]) + 2048 * 4) for i in range(NSLOT)]
        hhi_t = nc.alloc_sbuf_tensor_at("hhi", [128, 8, NT], F32R, offset=_addr(yc))
        hlo_t = nc.alloc_sbuf_tensor_at("hlo", [128, 8, NT], F32R, offset=_addr(hT))

        hactlo_t = nc.alloc_sbuf_tensor_at("hactlo", [128, NFF, NT], F32R, offset=_addr(big))
        acch_t = [nc.alloc_sbuf_tensor_at("acch%d" % i, [128, NT], F32R, offset=_addr(acc[i][0])) for i in range(4)]
        zDh_t = nc.alloc_sbuf_tensor_at("zDh", [128, 2, 2 + NT], F32R, offset=_addr(zD))
        HACT_HI = ([(acch_t[i][:, :], acc[i][1]) for i in range(4)] + [(zDh_t[:, c, 0:NT], zD_r0) for c in range(2)]
                   + [(hhi_t[:, c, :], yc_r[c]) for c in range(8)] + [(hlo_t[:, c, :], hT_r[c]) for c in range(8)])

        def wlo(i):
            return wlo_t[i][:, :].rearrange("p (k c) -> p k c", k=8)

        def whi(i):
            return whi_t[i][:, :].rearrange("p (k c) -> p k c", k=8)

        wseq = []

        def layer_loads(l, split, with_ada1):
            seq = []
            for a in range(8):
                seq.append(("in", l, a, split))
                if with_ada1:
                    seq.append(("ada", 0, 4 + a, False))
            for a in range(4):
                seq.append(("out", l, a, split))
            for cg in range(11):
                seq.append(("gate", l, cg, split))
                seq.append(("up", l, cg, split))
                if with_ada1:
                    seq.append(("ada", 1, cg, False))
            if with_ada1:
                seq.append(("ada", 1, 11, False))
            if split:
                for a in range(16):
                    seq.append(("downh", l, a, True))
            else:
                for ob in range(8):
                    seq.append(("down", l, ob, False))
            return seq

        for j in range(4):
            wseq.append(("ada", 0, j, False))
        for t in range(NTILES + 1):
            for l in range(2):
                wseq.extend(layer_loads(l, t < NTILES, t == 0 and l == 0))
        wpos = [0]
        wiss = [0]
        wsplit_done = [0]

        def w_issue(j):
            kind, l, a, _ = wseq[j]
            s = j % NSLOT
            if kind == "ada":
                o = wview8(s, 512); i = w_ada[l][:, 512 * a:512 * a + 512].rearrange("(k p) c -> p k c", p=128)
            elif kind == "in":
                o = wraw(s); i = w_in[l][:, 256 * a:256 * a + 256].rearrange("(k p) c -> p k c", p=128)
            elif kind == "out":
                o = wraw(s); i = w_out[l][:, 256 * a:256 * a + 256].rearrange("(k p) c -> p k c", p=128)
            elif kind in ("gate", "up"):
                src = w_gate if kind == "gate" else w_up
                o = wraw(s); i = src[l][:, 256 * a:256 * a + 256].rearrange("(k p) c -> p k c", p=128)
            elif kind == "downh":
                ob_, hf_ = divmod(a, 2)
                o = wsl[s][:, 0:1408].rearrange("p (k c) -> p k c", k=11)
                i = w_down[l][1408 * hf_:1408 * hf_ + 1408, 128 * ob_:128 * ob_ + 128].rearrange("(k p) c -> p k c", p=128)
            else:
                o = wview22(s); i = w_down[l][:, 128 * a:128 * a + 128].rearrange("(k p) c -> p k c", p=128)
            P.dma(sp, wsl_d[s], [(o, i)], writes=[wsl_r[s]])

        def w_split(j):
            if not wseq[j][3]:
                return
            s = j % NSLOT
            nw = 1408 if wseq[j][0] == "downh" else 2048
            raw = wsl[s][:, 0:nw]
            P.op(act, lambda: S.copy(out=whi_t[s][:, 0:nw], in_=raw), reads=[wsl_r[s]], writes=[wsl_r[s]])
            P.op(dve, lambda: V.tensor_tensor(out=wlo_t[s][:, 0:nw], in0=raw, in1=whi_t[s][:, 0:nw].bitcast(F32), op=ALU.subtract),
                 reads=[wsl_r[s]], writes=[wsl_r[s]])

        def wnext(kind, l, a):
            j = wpos[0]
            assert wseq[j][:3] == (kind, l, a), (wseq[j], kind, l, a)
            while wiss[0] < min(j + NSLOT, len(wseq)):
                w_issue(wiss[0])
                wiss[0] += 1
            while wsplit_done[0] < min(j + 2, wiss[0]):
                w_split(wsplit_done[0])
                wsplit_done[0] += 1
            wpos[0] += 1
            s = j % NSLOT
            return s, wsl_r[s]

        P.dma(pool, d_misc, [(ident[:], c_ident), (tril[:], c_tril), (invw[:], c_invw), (invcnt[:], c_invcnt)],
              writes=[ident_r, tril_r, invw_r, invcnt_r])
        P.op(dve, lambda: V.memset(ones[:], 1.0), writes=[ones_r])

        def mm_group(out_ap, out_reg, pairs, reads):
            n = len(pairs)
            for i, (l_ap, r_ap) in enumerate(pairs):
                P.op(pe, (lambda l_ap=l_ap, r_ap=r_ap, i=i: T.matmul(out_ap, l_ap, r_ap, start=(i == 0), stop=(i == n - 1))),
                     reads=reads, writes=[out_reg], inc=(i == n - 1))

        def transpose(out_ap, out_reg, in_ap, in_regs, npart, inc=True):
            P.op(pe, lambda: T.transpose(out_ap, in_ap, ident[0:npart, 0:npart]),
                 reads=list(in_regs) + [ident_r], writes=[out_reg], inc=inc)

        for l in range(2):
            st0, st0_r = stg[0]
            st1, st1_r = stg[1]

            def rows(v):
                return v.rearrange("(c p) -> c p", p=128)

            pairs = [
                (st0[C_BADA:C_BADA + 48, 0:128], rows(b_ada[l])),
                (st0[C_GPM:C_GPM + 8, 0:128], rows(g_pre_mix[l])),
                (st0[C_GOM:C_GOM + 8, 0:128], rows(g_post_mix[l])),
                (st0[C_GPF:C_GPF + 8, 0:128], rows(g_pre_ffn[l])),
                (st0[C_GOF:C_GOF + 8, 0:128], rows(g_post_ffn[l])),
                (st0[C_CAB:C_CAB + 2, 0:128], rows(conv_a_b[l])),
                (st0[C_LAG:C_LAG + 2, 0:128], rows(ln_a_g[l])),
                (st0[C_LAB:C_LAB + 2, 0:128], rows(ln_a_b[l])),
                (st0[C_PSC:C_PSC + 2, 0:128], rows(pool_scale[l])),
            ]
            for h in range(2):
                pairs.append((st0[C_CDW + 3 * h:C_CDW + 3 * h + 3, 0:128], conv_d_w[l][:, 128 * h:128 * h + 128]))
            P.dma(pool, stg_d[0], pairs, writes=[st0_r])
            pairs = []
            for h in range(2):
                pairs.append((st1[31 * h:31 * h + 31, 0:128], conv_a_w[l][:, 128 * h:128 * h + 128]))
            P.dma(pool, stg_d[1], pairs, writes=[st1_r])
            bk, bk_r = nb()
            transpose(bk[:, 0:NS1], bk_r, st0[0:NS1, 0:128], [st0_r], NS1, inc=False)
            transpose(bk[:, NS1:NS1 + 62], bk_r, st1[0:62, 0:128], [st1_r], 62)
            pT, pT_r = parT[l]
            P.op(dve, lambda: V.tensor_copy(pT[:, 0:NPAR], bk[:, 0:NPAR]), reads=[bk_r], writes=[pT_r])
            g_, g_r = gcb[l]
            b_, b_r = bsb[l]
            w0_, w0_r = w00[l]
            pairs = [(g_[:, 0, :], ln_c_g[l].partition_broadcast(128)), (g_[:, 1, :], ln_c_b[l].partition_broadcast(128))]
            for h in range(4):
                rs_ = slice(64 * (h % 2), 64 * (h % 2) + 64)
                pairs.append((b_[rs_, h // 2, :], sgu_b[l, h].partition_broadcast(64)))
                pairs.append((w0_[rs_, h // 2:h // 2 + 1], sgu_w[l, h, 0, 0:1].partition_broadcast(64)))
            P.dma(pool, P.dsem("par_a%d" % l), pairs, writes=[g_r, b_r, w0_r])
            bd_, bd_r = BD[l]
            P.op(dve, lambda: V.memset(bd_[:], 0.0), writes=[bd_r])
            pairs = []
            for g in range(4):
                r0 = 64 * (g % 2)
                pairs.append((bd_[r0:r0 + 64, g // 2, r0:r0 + 64], pool_w[l, g]))
            P.dma(pool, P.dsem("par_b%d" % l), pairs, writes=[bd_r])
            wm_, wm_r = wmT[l]
            for h in range(4):
                s_, s_r = stg[h % 2]
                P.dma(pool, stg_d[h % 2], [(s_[:, 0:128], sgu_w[l, h])], writes=[s_r])
                P.op(dve, lambda: V.tensor_tensor(out=s_[:, 0:128], in0=s_[:, 0:128], in1=tril[:], op=ALU.mult),
                     reads=[s_r, tril_r], writes=[s_r])
                bk, bk_r = nb()
                transpose(bk[:, 0:128], bk_r, s_[:, 0:128], [s_r], 128)
                P.op(dve, lambda: V.tensor_copy(wm_[:, h, :], bk[:, 0:128]), reads=[bk_r], writes=[wm_r])

        s_, s_r = stg[0]
        P.dma(pool, stg_d[0], [(s_[0:NS + 1, :], cc)], writes=[s_r])
        P.op(act, lambda: S.activation(out=s_[0:NS + 1, :], in_=s_[0:NS + 1, :], func=AF.Silu), reads=[s_r], writes=[s_r])
        bk, bk_r = nb()
        for dc in range(8):
            transpose(bk[:, dc * (NS + 1):(dc + 1) * (NS + 1)], bk_r, s_[0:NS + 1, dc * 128:(dc + 1) * 128], [s_r], NS + 1, inc=(dc == 7))
        P.op(dve, lambda: V.tensor_copy(scT[:].rearrange("p a b -> p (a b)"), bk[:, 0:8 * (NS + 1)]), reads=[bk_r], writes=[scT_r])
        NC1 = NS + 1

        def mod_load(l, j):
            mT_, mT_r = modT[l]
            pT, pT_r = parT[l]
            if True:
                s, s_r = wnext("ada", l, j)
                wv = wview8(s, 512)
                bkt, bkt_r = nb()
                mm_group(bkt[0:NC1, 0:512], bkt_r, [(scT[:, kc, :], wv[:, kc, :]) for kc in range(8)], reads=[s_r, scT_r])
                tk, tk_r = tm[j % 2]
                P.op(act, lambda: S.copy(out=tk[0:NC1, 0:512], in_=bkt[0:NC1, 0:512]), reads=[bkt_r], writes=[tk_r])
                bk, bk_r = nb()
                for q in range(4):
                    transpose(bk[:, q * NC1:(q + 1) * NC1], bk_r, tk[0:NC1, q * 128:(q + 1) * 128], [tk_r], NC1, inc=(q == 3))
                fc0 = 4 * j
                P.op(dve, lambda: V.tensor_tensor(out=mT_[:, fc0:fc0 + 4, :],
                                                  in0=bk[:, 0:4 * NC1].rearrange("p (a b) -> p a b", a=4),
                                                  in1=pT[:, C_BADA + fc0:C_BADA + fc0 + 4].unsqueeze(2).broadcast_to([128, 4, NC1]),
                                                  op=ALU.add),
                     reads=[bk_r, pT_r], writes=[mT_r])

        def mod_finish(l, which=("A1", "G1", "A2", "G2")):
            mT_, mT_r = modT[l]
            pT, pT_r = parT[l]
            for dc in range(8):
                if "A1" in which:
                    P.op(dve, lambda: V.tensor_scalar(out=A1[l][0][:, dc, :], in0=mT_[:, 8 + dc, :], scalar1=1.0, scalar2=pT[:, C_GPM + dc:C_GPM + dc + 1],
                                                      op0=ALU.add, op1=ALU.mult), reads=[mT_r, pT_r], writes=[A1[l][1]])
                if "G1" in which:
                    P.op(dve, lambda: V.tensor_scalar(out=G1[l][0][:, dc, :], in0=mT_[:, 16 + dc, :], scalar1=pT[:, C_GOM + dc:C_GOM + dc + 1], scalar2=None,
                                                      op0=ALU.mult), reads=[mT_r, pT_r], writes=[G1[l][1]])
                if "A2" in which:
                    P.op(dve, lambda: V.tensor_scalar(out=A2[l][0][:, dc, :], in0=mT_[:, 32 + dc, :], scalar1=1.0, scalar2=pT[:, C_GPF + dc:C_GPF + dc + 1],
                                                      op0=ALU.add, op1=ALU.mult), reads=[mT_r, pT_r], writes=[A2[l][1]])
                if "G2" in which:
                    P.op(dve, lambda: V.tensor_scalar(out=G2[l][0][:, dc, :], in0=mT_[:, 40 + dc, :], scalar1=pT[:, C_GOF + dc:C_GOF + dc + 1], scalar2=None,
                                                      op0=ALU.mult), reads=[mT_r, pT_r], writes=[G2[l][1]])

        for j in range(4):
            mod_load(0, j)
        mod_finish(0, ("A1",))

        def sample_stats(src8, src_regs, eps, dim):
            q, q_r = sq[0]
            qv = q[:, 0:8 * NS].rearrange("p (c n) -> p c n", c=8)
            P.op(act, lambda: S.activation(out=qv, in_=src8, func=AF.Square), reads=list(src_regs), writes=[q_r])
            bk, bk_r = nb()
            P.op(pe, lambda: T.matmul(bk[:, 0:8 * NS], ones[:], q[:, 0:8 * NS], start=True, stop=True), reads=[ones_r, q_r], writes=[bk_r], inc=True)
            P.op(dve, lambda: V.tensor_reduce(out=ssum[:, 0:NS], in_=bk[:, 0:8 * NS].rearrange("p (c n) -> p n c", c=8), axis=AX.X, op=ALU.add),
                 reads=[bk_r], writes=[ssum_r])
            P.op(act, lambda: S.activation(out=rstd[:, :NS], in_=ssum[:, 0:NS], func=AF.Sqrt, scale=1.0 / dim, bias=eps), reads=[ssum_r], writes=[rstd_r])
            P.op(dve, lambda: V.reciprocal(out=rstd[:, :NS], in_=rstd[:, :NS]), reads=[rstd_r], writes=[rstd_r])

        def rms_stats(src, src_regs, n, N, eps, dim):
            if N == NS and n == 8:
                sample_stats(src[:, 0:8, 0:NS], src_regs[0:8], eps, dim)
                return
            st_ = StatAccPE(N)
            for c in range(n):
                st_.square(src[:, c, :N], src_regs[c], "pool")
                st_.mm(last=(c == n - 1))
            st_.finish(eps, dim)

        class StatAccPE:
            def __init__(self, N, bank=None):
                if bank is None:
                    self.bk, self.bk_r = nb(hold=True)
                else:
                    self.bk, self.bk_r = bank
                    held.add(pb.index(self.bk))
                self.N = N
                self.k = 0
                self.n = 0
                self.pending = []

            def square(self, src_ap, src_reg, eng):
                q, q_r = sq[self.k % 2]
                self.k += 1
                N = self.N
                if eng == "pool":
                    P.op(pool, lambda: G.tensor_tensor(out=q[:, :N], in0=src_ap, in1=src_ap, op=ALU.mult), reads=[src_reg], writes=[q_r])
                else:
                    P.op(act, lambda: S.activation(out=q[:, :N], in_=src_ap, func=AF.Square), reads=[src_reg], writes=[q_r])
                self.pending.append((q, q_r))

            def mm(self, last=False):
                q, q_r = self.pending.pop(0)
                N = self.N
                first = (self.n == 0)
                self.n += 1
                P.op(pe, lambda: T.matmul(self.bk[:, :N], ones[:], q[:, :N], start=first, stop=last), reads=[ones_r, q_r], writes=[self.bk_r], inc=True)

            def finish(self, eps, dim):
                N = self.N
                assert not self.pending
                held.discard(pb.index(self.bk))
                P.op(act, lambda: S.activation(out=rstd[:, :N], in_=self.bk[:, :N], func=AF.Sqrt, scale=1.0 / dim, bias=eps),
                     reads=[self.bk_r], writes=[rstd_r])
                P.op(dve, lambda: V.reciprocal(out=rstd[:, :N], in_=rstd[:, :N]), reads=[rstd_r], writes=[rstd_r])

        class StatAcc:
            def __init__(self, N):
                self.N = N
                self.k = 0

            def square(self, src_ap, src_reg, eng):
                N = self.N
                first = (self.k == 0)
                q, q_r = (ssum, ssum_r) if first else sq[self.k % 2]
                self.k += 1
                if eng == "pool":
                    P.op(pool, lambda: G.tensor_tensor(out=q[:, :N], in0=src_ap, in1=src_ap, op=ALU.mult), reads=[src_reg], writes=[q_r])
                else:
                    P.op(act, lambda: S.activation(out=q[:, :N], in_=src_ap, func=AF.Square), reads=[src_reg], writes=[q_r])
                if not first:
                    P.op(pool, lambda: G.tensor_tensor(out=ssum[:, :N], in0=ssum[:, :N], in1=q[:, :N], op=ALU.add), reads=[ssum_r, q_r], writes=[ssum_r])

            def mm(self, last=False):
                pass

            def finish(self, eps, dim):
                N = self.N
                bk, bk_r = nb()
                P.op(pe, lambda: T.matmul(bk[:, :N], ones[:], ssum[:, :N], start=True, stop=True), reads=[ones_r, ssum_r], writes=[bk_r], inc=True)
                P.op(act, lambda: S.activation(out=rstd[:, :N], in_=bk[:, :N], func=AF.Sqrt, scale=1.0 / dim, bias=eps),
                     reads=[bk_r], writes=[rstd_r])
                P.op(dve, lambda: V.reciprocal(out=rstd[:, :N], in_=rstd[:, :N]), reads=[rstd_r], writes=[rstd_r])

        def modulate(l, Aq, Bbase, N, sample):
            A_, A_r = Aq
            mT_, mT_r = modT[l]
            if not sample:
                def s1(c):
                    t_, t_r = tm[c % 3]
                    P.op(dve, lambda: V.scalar_tensor_tensor(out=t_[:, :N], in0=xT[:, c, :N], scalar=A_[:, c, 0:1], in1=rstd[:, :N],
                                                             op0=ALU.mult, op1=ALU.mult),
                         reads=[xT_r[c], A_r, rstd_r], writes=[t_r])
                    P.op(act, lambda: S.activation(out=hhi_t[:, c, :N], in_=t_[:, :N], func=AF.Identity, bias=mT_[:, Bbase + c, 0:1], scale=1.0),
                         reads=[t_r, mT_r], writes=[yc_r[c]])

                def s2(c):
                    t_, t_r = tm[c % 3]
                    P.op(dve, lambda: V.scalar_tensor_tensor(out=hlo_t[:, c, :N], in0=t_[:, :N], scalar=mT_[:, Bbase + c, 0:1],
                                                             in1=hhi_t[:, c, :N].bitcast(F32), op0=ALU.add, op1=ALU.subtract),
                         reads=[t_r, mT_r, yc_r[c]], writes=[hT_r[c]])
                s1(0)
                for c in range(8):
                    if c + 1 < 8:
                        s1(c + 1)
                    s2(c)
                return
            t_, t_r = tm[0]
            tv = t_[:, 0:8 * NS].rearrange("p (c n) -> p c n", c=8)
            rb = rstd[:, 0:NS].unsqueeze(1).broadcast_to([128, 8, NS])
            P.op(dve, lambda: V.tensor_tensor(out=tv, in0=xT[:, :, 0:NS], in1=rb, op=ALU.mult), reads=xT_r + [rstd_r], writes=[t_r])
            P.op(dve, lambda: V.tensor_tensor(out=tv, in0=tv, in1=A_[:, :, 1:1 + NS], op=ALU.mult), reads=[t_r, A_r], writes=[t_r])
            P.op(dve, lambda: V.tensor_tensor(out=hT[:, :, 0:NS], in0=tv, in1=mT_[:, Bbase:Bbase + 8, 1:1 + NS], op=ALU.add),
                 reads=[t_r, mT_r], writes=hT_r)

        def resid_update(l, src, src_regs, Gq, N, sample, want_stats=False, src_aps=None, stat_bank=None):
            G_, G_r = Gq
            if sample:
                t_, t_r = tm[0]
                tv = t_[:, 0:8 * NS].rearrange("p (c n) -> p c n", c=8)
                rb = rstd[:, 0:NS].unsqueeze(1).broadcast_to([128, 8, NS])
                P.op(dve, lambda: V.tensor_tensor(out=tv, in0=src[:, 0:8, 0:NS], in1=rb, op=ALU.mult), reads=list(src_regs[0:8]) + [rstd_r], writes=[t_r])
                P.op(dve, lambda: V.tensor_tensor(out=tv, in0=tv, in1=G_[:, :, 1:1 + NS], op=ALU.mult), reads=[t_r, G_r], writes=[t_r])
                P.op(dve, lambda: V.tensor_tensor(out=xT[:, :, 0:NS], in0=xT[:, :, 0:NS], in1=tv, op=ALU.add), reads=xT_r + [t_r], writes=xT_r)
                if want_stats:
                    sample_stats(xT[:, :, 0:NS], xT_r, RMS_EPS, D)
                return
            st2 = StatAccPE(N, bank=stat_bank) if want_stats else None
            for c in range(8):
                t_, t_r = tm[c % 2]
                if not sample:
                    sap = src_aps[c] if src_aps is not None else src[:, c, :N]
                    P.op(dve, lambda: V.scalar_tensor_tensor(out=t_[:, :N], in0=sap, scalar=G_[:, c, 0:1], in1=rstd[:, :N],
                                                             op0=ALU.mult, op1=ALU.mult),
                         reads=[src_regs[c], G_r, rstd_r], writes=[t_r])
                else:
                    P.op(dve, lambda: V.tensor_tensor(out=t_[:, :N], in0=src[:, c, :N], in1=rstd[:, :N], op=ALU.mult),
                         reads=[src_regs[c], rstd_r], writes=[t_r])
                    P.op(dve, lambda: V.tensor_tensor(out=t_[:, :N], in0=t_[:, :N], in1=G_[:, c, 1:1 + N], op=ALU.mult),
                         reads=[t_r, G_r], writes=[t_r])
                P.op(pool, lambda: G.tensor_tensor(out=xT[:, c, :N], in0=xT[:, c, :N], in1=t_[:, :N], op=ALU.add),
                     reads=[xT_r[c], t_r], writes=[xT_r[c]])
                if st2 is not None:
                    st2.square(xT[:, c, :N], xT_r[c], "act")
                    st2.mm(last=(c == 7))
            if st2 is not None:
                st2.finish(RMS_EPS, D)

        def small_out(dst_ap, chunks, nrow):
            k = smi[0] % 2
            smi[0] += 1
            s_, s_r = sm[k]
            bk, bk_r = nb()
            for ch, (ap_, regs) in enumerate(chunks):
                transpose(bk[0:nrow, ch * 128:(ch + 1) * 128], bk_r, ap_, regs, 128, inc=(ch == 1))
            P.op(dve, lambda: V.tensor_copy(s_[0:nrow, 0:256], bk[0:nrow, 0:256]), reads=[bk_r], writes=[s_r])
            P.dma(pool, sm_d[k], [(dst_ap, s_[0:nrow, 0:256])], reads=[s_r], is_output=True)

        tki = [0]

        def proj_blocks(wv, s_r, blocks, src, src_regs, nk, N, sample, tok_copy=None, split=None, kc_major=False, kc_order=None):
            blocks = list(blocks)
            if not sample and kc_major:
                bks = [nb() for _ in blocks]
                order = list(kc_order) if kc_order is not None else list(range(nk))
                for oi, kc in enumerate(order):
                    for bi, j in enumerate(blocks):
                        bk, bk_r = bks[bi]
                        cs = slice(j * 128, (j + 1) * 128)
                        if split is not None:
                            w_h, w_l, x_h, x_l, x_regs = split
                            trip = [(w_h[:, kc, cs], x_h[:, kc, :N]), (w_h[:, kc, cs], x_l[:, kc, :N]), (w_l[:, kc, cs], x_h[:, kc, :N])]
                            rr = [s_r, x_regs[kc], x_regs[8 + kc]]
                        else:
                            trip = [(wv[:, kc, cs], src[:, kc, :N])]
                            rr = [s_r, src_regs[kc]]
                        for ti_, (l_ap, r_ap) in enumerate(trip):
                            first = (oi == 0 and ti_ == 0)
                            lastm = (oi == nk - 1 and ti_ == len(trip) - 1)
                            P.op(pe, (lambda l_ap=l_ap, r_ap=r_ap, first=first, lastm=lastm, bk=bk: T.matmul(bk[:, :N], l_ap, r_ap, start=first, stop=lastm)),
                                 reads=rr, writes=[bk_r], inc=(ti_ == len(trip) - 1))
                for bi, j in enumerate(blocks):
                    yield j, bks[bi][0][:, :N], bks[bi][1]
                return
            if not sample:
                for j in blocks:
                    bk, bk_r = nb()
                    if split is None:
                        mm_group(bk[:, :N], bk_r, [(wv[:, kc, j * 128:(j + 1) * 128], src[:, kc, :N]) for kc in range(nk)],
                                 reads=[s_r] + src_regs)
                    else:
                        w_h, w_l, x_h, x_l, x_regs = split
                        pairs = []
                        for kc in range(nk):
                            cs = slice(j * 128, (j + 1) * 128)
                            pairs += [(w_h[:, kc, cs], x_h[:, kc, :N]), (w_h[:, kc, cs], x_l[:, kc, :N]), (w_l[:, kc, cs], x_h[:, kc, :N])]
                        mm_group(bk[:, :N], bk_r, pairs, reads=[s_r] + x_regs)
                    yield j, bk[:, :N], bk_r
                return
            c0 = 0 if tok_copy is not None else min(blocks) * 128
            c1 = (max(blocks) + 1) * 128 if blocks else 256
            bk, bk_r = nb()
            mm_group(bk[0:NS, c0:c1], bk_r, [(src[:, kc, 0:NS], wv[:, kc, c0:c1]) for kc in range(nk)], reads=[s_r] + src_regs)
            tk, tk_r = tm[tki[0] % 2]
            tki[0] += 1
            P.op(act, lambda: S.copy(out=tk[0:NS, c0:c1], in_=bk[0:NS, c0:c1]), reads=[bk_r], writes=[tk_r])
            if tok_copy is not None:
                tok_copy(tk, tk_r)
            if not blocks:
                return
            bk2, bk2_r = nb()
            for j in blocks:
                transpose(bk2[:, j * NS:(j + 1) * NS], bk2_r, tk[0:NS, j * 128:(j + 1) * 128], [tk_r], NS, inc=(j == blocks[-1]))
            for j in blocks:
                yield j, bk2[:, j * NS:(j + 1) * NS], bk2_r

        def block(l, ti, sample, have_stats):
            N = NS if sample else NT
            first = (ti == 0)
            last = (ti == NTILES - 1)
            pT, pT_r = parT[l]

            def pcol(c):
                return pT[:, c:c + 1]

            dB = [(big[:, 4, :], big_r[4]), (big[:, 5, :], big_r[5])]
            ntb = 1 if sample else 4
            mt = NS if sample else 128

            if not have_stats:
                rms_stats(xT, xT_r, 8, N, RMS_EPS, D)
            modulate(l, A1[l], 0, N, sample)

            h_hi = hhi_t[:]
            h_lo = hlo_t[:]

            def inproj(a):
                s, s_r = wnext("in", l, a)
                wv = wraw(s)
                sp_ = None if sample else (whi(s), wlo(s), h_hi, h_lo, hT_r + yc_r)
                blocks = [0, 1]
                tok_copy = None
                if a == 4:
                    blocks = []
                    if not sample:
                        w_h, w_l = whi(s), wlo(s)
                        for tb in range(4):
                            bk, bk_r = nb()
                            ts_ = slice(tb * 128, tb * 128 + 128)
                            pairs = []
                            for kc in range(8):
                                pairs += [(h_hi[:, kc, ts_], w_h[:, kc, :]), (h_lo[:, kc, ts_], w_h[:, kc, :]), (h_hi[:, kc, ts_], w_l[:, kc, :])]
                            mm_group(bk[:, 0:256], bk_r, pairs, reads=[s_r] + hT_r + yc_r)
                            P.op(act, lambda: S.copy(out=cv[:, tb, :], in_=bk[:, 0:256]), reads=[bk_r], writes=cv_r)
                        return
                    else:
                        def tok_copy(tk, tk_r):
                            P.op(pool, lambda: G.tensor_copy(out=cv[0:NS, 0, :], in_=tk[0:NS, 0:256]), reads=[tk_r], writes=cv_r)
                for j, ps, ps_r in proj_blocks(wv, s_r, blocks, hT, hT_r, 8, N, sample, tok_copy=tok_copy, split=sp_, kc_major=(a == 0)):
                    cb = 2 * a + j
                    if cb in (2, 3):
                        P.op(act, lambda: S.activation(out=big[:, cb, :N], in_=ps, func=AF.Sigmoid), reads=[ps_r], writes=[big_r[cb]])
                    elif cb in (4, 5):
                        P.op(act, lambda: S.copy(out=zB[:, cb - 4, 15:15 + N], in_=ps), reads=[ps_r], writes=zB_r)
                    else:
                        P.op(act, lambda: S.copy(out=big[:, cb, :N], in_=ps), reads=[ps_r], writes=[big_r[cb]])

            def mixA1():
                if sample:
                    yield
                hA, hA_r = haloA[l]
                for ch in range(2):
                    P.op(dve, lambda: V.tensor_tensor(out=zA[:, ch, 30:30 + N], in0=big[:, ch, :N], in1=big[:, 2 + ch, :N], op=ALU.mult),
                         reads=[big_r[ch], big_r[2 + ch]], writes=zA_r)
                if not sample:
                    if first:
                        P.op(pool, lambda: G.memset(zA[:, :, 0:30], 0.0), writes=zA_r)
                    else:
                        P.op(pool, lambda: G.tensor_copy(out=zA[:, :, 0:30], in_=hA[:]), reads=[hA_r], writes=zA_r)
                    for k in range(31):
                        if k in (8, 16, 24):
                            yield
                        for ch in range(2):
                            a_, a_r = acc[2 * ch + (k % 2)]
                            wc = pcol(C_CAW + 31 * ch + k)
                            if k < 2:
                                if k == 0:
                                    P.op(dve, lambda: V.tensor_scalar(out=a_[:, :N], in0=zA[:, ch, k:k + N], scalar1=wc, scalar2=pcol(C_CAB + ch),
                                                                      op0=ALU.mult, op1=ALU.add), reads=zA_r + [pT_r], writes=[a_r])
                                else:
                                    P.op(dve, lambda: V.tensor_scalar(out=a_[:, :N], in0=zA[:, ch, k:k + N], scalar1=wc, scalar2=None,
                                                                      op0=ALU.mult), reads=zA_r + [pT_r], writes=[a_r])
                            else:
                                P.op(dve, lambda: V.scalar_tensor_tensor(out=a_[:, :N], in0=zA[:, ch, k:k + N], scalar=wc, in1=a_[:, :N],
                                                                         op0=ALU.mult, op1=ALU.add), reads=zA_r + [pT_r, a_r], writes=[a_r])
                    for ch in range(2):
                        P.op(pool, lambda: G.tensor_tensor(out=acc[2 * ch][0][:, :N], in0=acc[2 * ch][0][:, :N], in1=acc[2 * ch + 1][0][:, :N], op=ALU.add),
                             reads=[acc[2 * ch][1], acc[2 * ch + 1][1]], writes=[acc[2 * ch][1]])
                    if last:
                        small_out(nap[l], [(zA[:, ch, N:N + 30], zA_r) for ch in range(2)], 30)
                    else:
                        P.op(pool, lambda: G.tensor_copy(out=hA[:], in_=zA[:, :, N:N + 30]), reads=zA_r, writes=[hA_r])
                else:
                    for ch in range(2):
                        a_, a_r = acc[2 * ch]
                        P.op(dve, lambda: V.tensor_tensor(out=prd[:].rearrange("p (s k) -> p s k", s=NS),
                                                          in0=stAT[:, ch, :].rearrange("p (s k) -> p s k", s=NS),
                                                          in1=pT[:, C_CAW + 31 * ch:C_CAW + 31 * ch + 30].unsqueeze(1).broadcast_to([128, NS, 30]),
                                                          op=ALU.mult), reads=[stAT_r, pT_r], writes=[prd_r])
                        P.op(dve, lambda: V.tensor_reduce(out=red[:, ch, :], in_=prd[:].rearrange("p (s k) -> p s k", s=NS), axis=AX.X, op=ALU.add),
                             reads=[prd_r], writes=[red_r])
                        P.op(dve, lambda: V.scalar_tensor_tensor(out=a_[:, :N], in0=zA[:, ch, 30:30 + N], scalar=pcol(C_CAW + 31 * ch + 30), in1=red[:, ch, :],
                                                                 op0=ALU.mult, op1=ALU.add), reads=zA_r + [pT_r, red_r], writes=[a_r])
                        P.op(dve, lambda: V.tensor_scalar(out=a_[:, :N], in0=a_[:, :N], scalar1=pcol(C_CAB + ch), scalar2=None, op0=ALU.add),
                             reads=[a_r, pT_r], writes=[a_r])
                    small_out(nas[l, :, 29, :], [(zA[:, ch, 30:30 + N], zA_r) for ch in range(2)], NS)

            def mixA2a():
                bk, bk_r = nb()
                mm_group(bk[:, :N], bk_r, [(ones[:], acc[2 * ch][0][:, :N]) for ch in range(2)], reads=[ones_r, acc[0][1], acc[2][1]])
                for ch in range(2):
                    a_, a_r = acc[2 * ch]
                    P.op(dve, lambda: V.scalar_tensor_tensor(out=a_[:, :N], in0=bk[:, :N], scalar=-1.0 / 256, in1=a_[:, :N], op0=ALU.mult, op1=ALU.add),
                         reads=[bk_r, a_r], writes=[a_r])
                    P.op(pool, lambda: G.tensor_tensor(out=acc[2 * ch + 1][0][:, :N], in0=a_[:, :N], in1=a_[:, :N], op=ALU.mult),
                         reads=[a_r], writes=[acc[2 * ch + 1][1]])

            def mixA2b():
                bk2, bk2_r = nb()
                mm_group(bk2[:, :N], bk2_r, [(ones[:], acc[2 * ch + 1][0][:, :N]) for ch in range(2)], reads=[ones_r, acc[1][1], acc[3][1]])
                t0, t0_r = tm[2]
                P.op(act, lambda: S.activation(out=t0[:, :N], in_=bk2[:, :N], func=AF.Sqrt, scale=1.0 / 256, bias=LN_EPS), reads=[bk2_r], writes=[t0_r])
                P.op(dve, lambda: V.reciprocal(out=t0[:, :N], in_=t0[:, :N]), reads=[t0_r], writes=[t0_r])
                for ch in range(2):
                    a_, a_r = acc[2 * ch]
                    P.op(dve, lambda: V.tensor_tensor(out=a_[:, :N], in0=a_[:, :N], in1=t0[:, :N], op=ALU.mult), reads=[a_r, t0_r], writes=[a_r])
                    P.op(act, lambda: S.activation(out=yc[:, ch, :N], in_=a_[:, :N], func=AF.Silu, scale=pcol(C_LAG + ch), bias=pcol(C_LAB + ch)),
                         reads=[a_r, pT_r], writes=[yc_r[ch]])


            def mixB1():
                hB, hB_r = haloB[l]
                dB = [(big[:, 4, :], big_r[4]), (big[:, 5, :], big_r[5])]
                if not sample:
                    E = 15 + N
                    if first:
                        P.op(pool, lambda: G.memset(zB[:, :, 0:15], 0.0), writes=zB_r)
                    else:
                        P.op(pool, lambda: G.tensor_copy(out=zB[:, :, 0:15], in_=hB[:]), reads=[hB_r], writes=zB_r)
                    (P0, P0_r), (Q0, Q0_r), (P1, P1_r), (Q1, Q1_r) = PQ

                    def shadd(o, o_r, i, i_r, sh, lo, rows):
                        P.op(dve, lambda: V.tensor_tensor(out=o[rows, lo:E], in0=i[rows, lo:E], in1=i[rows, lo - sh:E - sh], op=ALU.add),
                             reads=[i_r], writes=[o_r])
                    al = slice(0, 128); hi = slice(64, 128); lo_ = slice(0, 64)
                    P.op(dve, lambda: V.tensor_tensor(out=P0[:, 1:E], in0=zB[:, 0, 1:E], in1=zB[:, 0, 0:E - 1], op=ALU.add), reads=zB_r, writes=[P0_r])
                    shadd(Q0, Q0_r, P0, P0_r, 2, 3, hi)
                    P.op(dve, lambda: V.tensor_tensor(out=P1[:, 1:E], in0=zB[:, 1, 1:E], in1=zB[:, 1, 0:E - 1], op=ALU.add), reads=zB_r, writes=[P1_r])
                    shadd(Q1, Q1_r, P1, P1_r, 2, 3, al)
                    shadd(P1, P1_r, Q1, Q1_r, 4, 7, al)
                    shadd(Q1, Q1_r, P1, P1_r, 8, 15, hi)
                    srcs = [[(P0, P0_r, lo_), (Q0, Q0_r, hi)], [(P1, P1_r, lo_), (Q1, Q1_r, hi)]]
                    for ch in range(2):
                        d_, d_r = dB[ch]
                        for (s_, s_r, rows) in srcs[ch]:
                            P.op(dve, lambda: V.scalar_tensor_tensor(out=d_[rows, :N], in0=s_[rows, 15:E], scalar=invw[rows, ch:ch + 1], in1=zB[rows, ch, 15:E],
                                                                     op0=ALU.mult, op1=ALU.subtract), reads=[s_r, invw_r] + zB_r, writes=[d_r])
                            if first:
                                t1, t1_r = tm[2]
                                P.op(dve, lambda: V.tensor_tensor(out=t1[rows, 0:16], in0=s_[rows, 15:31], in1=invcnt[rows, ch, :], op=ALU.mult),
                                     reads=[s_r, invcnt_r], writes=[t1_r])
                                P.op(dve, lambda: V.tensor_tensor(out=d_[rows, 0:16], in0=t1[rows, 0:16], in1=zB[rows, ch, 15:31], op=ALU.subtract),
                                     reads=[t1_r, d_r] + zB_r, writes=[d_r])
                    if last:
                        small_out(nbp[l], [(zB[:, ch, N:N + 15], zB_r) for ch in range(2)], 15)
                    else:
                        P.op(pool, lambda: G.tensor_copy(out=hB[:], in_=zB[:, :, N:N + 15]), reads=zB_r, writes=[hB_r])
                else:
                    for ch in range(2):
                        d_, d_r = dB[ch]
                        for half in range(2):
                            rows = slice(64 * half, 64 * half + 64)
                            w = POOLW[2 * ch + half]
                            k0 = 15 - (w - 1)
                            P.op(dve, lambda: V.tensor_reduce(out=red[rows, ch, :], in_=stBT[rows, ch, :].rearrange("p (s k) -> p s k", s=NS)[:, :, k0:15],
                                                              axis=AX.X, op=ALU.add), reads=[stBT_r], writes=[red_r])
                        t1, t1_r = tm[2]
                        P.op(dve, lambda: V.tensor_tensor(out=t1[:, :N], in0=red[:, ch, :], in1=zB[:, ch, 15:15 + N], op=ALU.add),
                             reads=[red_r] + zB_r, writes=[t1_r])
                        P.op(dve, lambda: V.scalar_tensor_tensor(out=d_[:, :N], in0=t1[:, :N], scalar=invw[:, ch:ch + 1], in1=zB[:, ch, 15:15 + N],
                                                                 op0=ALU.mult, op1=ALU.subtract), reads=[t1_r, invw_r] + zB_r, writes=[d_r])
                    small_out(nbs[l, :, 14, :], [(zB[:, ch, 15:15 + N], zB_r) for ch in range(2)], NS)

            def mixB2():
                bd_, bd_r = BD[l]
                for ch in range(2):
                    d_, d_r = dB[ch]
                    bk, bk_r = nb()
                    mm_group(bk[:, :N], bk_r, [(bd_[:, ch, :], d_[:, :N])], reads=[bd_r, d_r])
                    P.op(act, lambda: S.activation(out=yc[:, 2 + ch, :N], in_=bk[:, :N], func=AF.Identity, scale=pcol(C_PSC + ch)),
                         reads=[bk_r, pT_r], writes=[yc_r[2 + ch]])


            def mixC1():
                g_, g_r = gcb[l]
                ntb = 1 if sample else 4
                mt = NS if sample else 128
                for tb in range(ntb):
                    P.op(dve, lambda: V.bn_stats(out=bst[0:mt, tb, :], in_=cv[0:mt, tb, :]), reads=cv_r, writes=[bst_r])
                    P.op(dve, lambda: V.bn_aggr(out=bmv[0:mt, tb, :], in_=bst[0:mt, tb, :]), reads=[bst_r], writes=[bmv_r])
                P.op(act, lambda: S.activation(out=brs[0:mt, 0:ntb], in_=bmv[0:mt, 0:ntb, 1], func=AF.Sqrt, scale=1.0, bias=LN_EPS), reads=[bmv_r], writes=[brs_r])
                P.op(dve, lambda: V.reciprocal(out=brs[0:mt, 0:ntb], in_=brs[0:mt, 0:ntb]), reads=[brs_r], writes=[brs_r])
                for tb in range(ntb):
                    P.op(dve, lambda: V.tensor_scalar(out=cv[0:mt, tb, :], in0=cv[0:mt, tb, :], scalar1=bmv[0:mt, tb, 0:1], scalar2=brs[0:mt, tb:tb + 1],
                                                      op0=ALU.subtract, op1=ALU.mult), reads=cv_r + [bmv_r, brs_r], writes=cv_r)
                P.op(pool, lambda: G.tensor_tensor(out=cv[0:mt, 0:ntb, :], in0=cv[0:mt, 0:ntb, :], in1=g_[0:mt, 0, :].unsqueeze(1).broadcast_to([mt, ntb, 256]), op=ALU.mult),
                     reads=cv_r + [g_r], writes=cv_r)
                P.op(pool, lambda: G.tensor_tensor(out=cv[0:mt, 0:ntb, :], in0=cv[0:mt, 0:ntb, :], in1=g_[0:mt, 1, :].unsqueeze(1).broadcast_to([mt, ntb, 256]), op=ALU.add),
                     reads=cv_r + [g_r], writes=cv_r)

            def mixC2():
                b_, b_r = bsb[l]
                t1, t1_r = tm[2]
                if not sample:
                    wm_, wm_r = wmT[l]
                    for h in range(4):
                        ch = h // 2
                        rows = slice(64 * (h % 2), 64 * (h % 2) + 64)
                        bk, bk_r = nb()
                        for tb in range(4):
                            P.op(pe, lambda: T.matmul(bk[:, tb * 128:(tb + 1) * 128], cv[:, tb, ch * 128:(ch + 1) * 128], wm_[:, h, :], start=True, stop=True),
                                 reads=cv_r + [wm_r], writes=[bk_r], inc=(tb == 3))
                        P.op(dve, lambda: V.tensor_tensor(out=t1[rows, 0:NT].rearrange("p (a b) -> p a b", a=4), in0=bk[rows, :].rearrange("p (a b) -> p a b", a=4),
                                                          in1=b_[rows, ch, :].unsqueeze(1).broadcast_to([64, 4, 128]), op=ALU.add),
                             reads=[bk_r, b_r], writes=[t1_r])
                        P.op(pool, lambda: G.tensor_tensor(out=yc[rows, 4 + ch, :N], in0=t1[rows, :N], in1=big[rows, 6 + ch, :N], op=ALU.mult),
                             reads=[t1_r, big_r[6 + ch]], writes=[yc_r[4 + ch]])
                else:
                    P.dma(pool, P.dsem("nvs%d" % l), [(nvs[l], cv[0:NS, 0, :])], reads=cv_r, is_output=True)
                    w0_, w0_r = w00[l]
                    bk, bk_r = nb()
                    for ch in range(2):
                        P.op(pe, lambda: T.transpose(bk[:, ch * NS:(ch + 1) * NS], cv[0:NS, 0, ch * 128:(ch + 1) * 128], ident[0:NS, 0:NS]),
                             reads=cv_r + [ident_r], writes=[bk_r], inc=(ch == 1))
                    for ch in range(2):
                        P.op(dve, lambda: V.tensor_scalar(out=t1[:, :N], in0=bk[:, ch * NS:(ch + 1) * NS], scalar1=w0_[:, ch:ch + 1], scalar2=b_[:, ch, 0:1],
                                                          op0=ALU.mult, op1=ALU.add), reads=[bk_r, w0_r, b_r], writes=[t1_r])
                        P.op(dve, lambda: V.tensor_tensor(out=yc[:, 4 + ch, :N], in0=t1[:, :N], in1=big[:, 6 + ch, :N], op=ALU.mult),
                             reads=[t1_r, big_r[6 + ch]], writes=[yc_r[4 + ch]])


            def mixD():
                hD, hD_r = haloD[l]
                for ch in range(2):
                    P.op(pool, lambda: G.tensor_tensor(out=zD[:, ch, 2:2 + N], in0=big[:, 12 + ch, :N], in1=big[:, 14 + ch, :N], op=ALU.mult),
                         reads=[big_r[12 + ch], big_r[14 + ch]], writes=zD_r)
                if not sample:
                    if first:
                        P.op(pool, lambda: G.memset(zD[:, :, 0:2], 0.0), writes=zD_r)
                    else:
                        P.op(pool, lambda: G.tensor_copy(out=zD[:, :, 0:2], in_=hD[:]), reads=[hD_r], writes=zD_r)
                for ch in range(2):
                    a_, a_r = acc[2 * ch + 1]
                    if not sample:
                        ins = [zD[:, ch, k:k + N] for k in range(3)]
                        rr = list(zD_r)
                    else:
                        sv = stDT[:, ch, :].rearrange("p (s k) -> p s k", s=NS)
                        ins = [sv[:, :, 0], sv[:, :, 1], zD[:, ch, 2:2 + N]]
                        rr = zD_r + [stDT_r]
                    P.op(dve, lambda: V.tensor_scalar(out=a_[:, :N], in0=ins[0], scalar1=pcol(C_CDW + 3 * ch), scalar2=None, op0=ALU.mult),
                         reads=rr + [pT_r], writes=[a_r])
                    for k in (1, 2):
                        P.op(dve, lambda: V.scalar_tensor_tensor(out=a_[:, :N], in0=ins[k], scalar=pcol(C_CDW + 3 * ch + k), in1=a_[:, :N], op0=ALU.mult, op1=ALU.add),
                             reads=rr + [pT_r, a_r], writes=[a_r])
                    P.op(pool, lambda: G.tensor_tensor(out=yc[:, 6 + ch, :N], in0=a_[:, :N], in1=big[:, 10 + ch, :N], op=ALU.mult),
                         reads=[a_r, big_r[10 + ch]], writes=[yc_r[6 + ch]])
                if not sample:
                    if last:
                        small_out(ndp[l], [(zD[:, ch, N:N + 2], zD_r) for ch in range(2)], 2)
                    else:
                        P.op(pool, lambda: G.tensor_copy(out=hD[:], in_=zD[:, :, N:N + 2]), reads=zD_r, writes=[hD_r])
                else:
                    small_out(nds[l, :, 1, :], [(zD[:, ch, 2:2 + N], zD_r) for ch in range(2)], NS)


            gA = mixA1()

            def stepA():
                for _ in gA:
                    return
            def inproj_m(a):
                inproj(a)
                if ti == 0 and l == 0:
                    mod_load(0, 4 + a)
                    if a == 1:
                        mod_finish(0, ("G1",))
                    if a == 7:
                        mod_finish(0, ("A2", "G2"))
            inproj_m(0); inproj_m(1); stepA(); inproj_m(2); stepA(); inproj_m(3); stepA(); inproj_m(4); stepA()
            for _ in gA:
                pass
            def ycat_split(c):
                if sample:
                    return
                P.op(act, lambda: S.copy(out=hlo_t[:, c, :N], in_=yc[:, c, :N]), reads=[yc_r[c]], writes=[hT_r[c]])
                P.op(dve, lambda: V.tensor_tensor(out=hhi_t[:, c, :N], in0=yc[:, c, :N], in1=hlo_t[:, c, :N].bitcast(F32), op=ALU.subtract),
                     reads=[yc_r[c], hT_r[c]], writes=[yc_r[c]])
            inproj_m(5); mixB1(); inproj_m(6); mixA2a(); mixC1(); inproj_m(7)
            mixA2b(); mixD(); ycat_split(6); ycat_split(7); mixB2(); ycat_split(2); ycat_split(3); mixC2(); ycat_split(4); ycat_split(5)
            ycat_split(0); ycat_split(1)

            korder = [6, 7, 2, 3, 4, 5, 0, 1]
            st = StatAcc(N)
            for a in range(4):
                s, s_r = wnext("out", l, a)
                sp_ = None if sample else (whi(s), wlo(s), hlo_t[:], hhi_t[:], yc_r + hT_r)
                for j, ps, ps_r in proj_blocks(wraw(s), s_r, range(2), yc, yc_r, 8, N, sample, split=sp_, kc_major=(a == 0), kc_order=korder):
                    ob = 2 * a + j
                    if ob % 2 == 0 and not sample:
                        P.op(dve, lambda: V.tensor_copy(big[:, ob, :N], ps), reads=[ps_r], writes=[big_r[ob]])
                    else:
                        P.op(act, lambda: S.copy(out=big[:, ob, :N], in_=ps), reads=[ps_r], writes=[big_r[ob]])
                    if not sample:
                        st.square(big[:, ob, :N], big_r[ob], "pool")
            if sample:
                sample_stats(big[:, 0:8, 0:NS], big_r[0:8], RMS_EPS, D)
            else:
                st.finish(RMS_EPS, D)
            resid_update(l, big, big_r, G1[l], N, sample, want_stats=True)

            def hact_split(kc):
                hi_ap, hi_r = HACT_HI[kc]
                P.op(act, lambda: S.copy(out=hi_ap[:, :N], in_=big[:, kc, :N]), reads=[big_r[kc]], writes=[hi_r])
                P.op(dve, lambda: V.tensor_tensor(out=hactlo_t[:, kc, :N], in0=big[:, kc, :N], in1=hi_ap[:, :N].bitcast(F32), op=ALU.subtract),
                     reads=[big_r[kc], hi_r], writes=[big_r[kc]])

            modulate(l, A2[l], 24, N, sample)
            for cg in range(11):
                sg, sg_r = wnext("gate", l, cg)
                sp_ = None if sample else (whi(sg), wlo(sg), h_hi, h_lo, hT_r + yc_r)
                for j, ps, ps_r in proj_blocks(wraw(sg), sg_r, range(2), hT, hT_r, 8, N, sample, split=sp_, kc_major=(cg == 0)):
                    fb = 2 * cg + j
                    P.op(act, lambda: S.activation(out=big[:, fb, :N], in_=ps, func=AF.Silu), reads=[ps_r], writes=[big_r[fb]])
                su, su_r = wnext("up", l, cg)
                sp_ = None if sample else (whi(su), wlo(su), h_hi, h_lo, hT_r + yc_r)
                for j, ps, ps_r in proj_blocks(wraw(su), su_r, range(2), hT, hT_r, 8, N, sample, split=sp_):
                    fb = 2 * cg + j
                    P.op(dve, lambda: V.tensor_tensor(out=big[:, fb, :N], in0=big[:, fb, :N], in1=ps, op=ALU.mult), reads=[big_r[fb], ps_r], writes=[big_r[fb]])
                    if not sample and fb < 6:
                        hact_split(fb)
                if ti == 0 and l == 0:
                    mod_load(1, cg)
            if ti == 0 and l == 0:
                mod_load(1, 11)
                mod_finish(1)
            st = StatAcc(N)
            tokb = None
            if not sample:
                for kc in range(6, NFF):
                    hact_split(kc)
                f_aps, f_regs = [], []
                bank0 = None
                for ob in range(8):
                    bk, bk_r = nb()
                    if ob == 0:
                        bank0 = (bk, bk_r)
                    for hf in range(2):
                        s, s_r = wnext("downh", l, 2 * ob + hf)
                        w_h = whi_t[s][:, 0:1408].rearrange("p (k c) -> p k c", k=11)
                        w_l = wlo_t[s][:, 0:1408].rearrange("p (k c) -> p k c", k=11)
                        pairs, rr = [], [s_r]
                        for k in range(11):
                            kc = 11 * hf + k
                            hi_ap, hi_r = HACT_HI[kc]
                            pairs += [(w_h[:, k, :], hi_ap[:, :N]), (w_h[:, k, :], hactlo_t[:, kc, :N]), (w_l[:, k, :], hi_ap[:, :N])]
                            rr += [hi_r, big_r[kc]]
                        n = len(pairs)
                        for i, (l_ap, r_ap) in enumerate(pairs):
                            P.op(pe, (lambda l_ap=l_ap, r_ap=r_ap, i=i: T.matmul(bk[:, :N], l_ap, r_ap, start=(hf == 0 and i == 0), stop=(hf == 1 and i == n - 1))),
                                 reads=rr, writes=[bk_r], inc=(i == n - 1))
                    if ob == 0:
                        t2, t2_r = tm[2]
                        P.op(dve, lambda: V.tensor_copy(t2[:, :N], bk[:, :N]), reads=[bk_r], writes=[t2_r])
                        f_aps.append(t2[:, :N]); f_regs.append(t2_r)
                        st.square(t2[:, :N], t2_r, "pool")
                    else:
                        f_aps.append(bk[:, :N]); f_regs.append(bk_r)
                        st.square(bk[:, :N], bk_r, "act")
                pbi[0] = pb.index(bank0[0])
                st.finish(RMS_EPS, D)
                resid_update(l, None, f_regs, G2[l], N, sample, want_stats=(l == 0), src_aps=f_aps, stat_bank=bank0)
                return
            for ob in range(8):
                s, s_r = wnext("down", l, ob)
                wv = wview22(s)
                if not sample:
                    bk, bk_r = nb()
                    mm_group(bk[:, :N], bk_r, [(wv[:, kc, :], big[:, kc, :N]) for kc in range(NFF)], reads=[s_r] + big_r)
                    pss = [(ob, bk[:, :N], bk_r)]
                else:
                    if ob % 4 == 0:
                        tokb = nb()
                    bk, bk_r = tokb
                    q = ob % 4
                    mm_group(bk[0:NS, q * 128:(q + 1) * 128], bk_r, [(big[:, kc, 0:NS], wv[:, kc, :]) for kc in range(NFF)], reads=[s_r] + big_r)
                    pss = []
                    if q == 3:
                        tk, tk_r = tm[tki[0] % 2]
                        tki[0] += 1
                        P.op(act, lambda: S.copy(out=tk[0:NS, 0:512], in_=bk[0:NS, 0:512]), reads=[bk_r], writes=[tk_r])
                        bk2, bk2_r = nb()
                        for qq in range(4):
                            transpose(bk2[:, qq * NS:(qq + 1) * NS], bk2_r, tk[0:NS, qq * 128:(qq + 1) * 128], [tk_r], NS, inc=(qq == 3))
                        pss = [(ob - 3 + qq, bk2[:, qq * NS:(qq + 1) * NS], bk2_r) for qq in range(4)]
                for (o2, ps, ps_r) in pss:
                    if o2 % 2 == 0 and not sample:
                        P.op(dve, lambda: V.tensor_copy(yc[:, o2, :N], ps), reads=[ps_r], writes=[yc_r[o2]])
                    else:
                        P.op(act, lambda: S.copy(out=yc[:, o2, :N], in_=ps), reads=[ps_r], writes=[yc_r[o2]])
            sample_stats(yc[:, :, 0:NS], yc_r, RMS_EPS, D)
            if DBG and l == 0 and ti == 0:
                P.dma(pool, P.dsem("dbg4"), [(dbg_h2, hT[:]), (dbg_f, yc[:]), (dbg_hact, big[:])], reads=yc_r + hT_r + big_r, is_output=True)
            resid_update(l, yc, yc_r, G2[l], N, sample, want_stats=(l == 0))
            if DBG and l == 0 and ti == 0:
                P.dma(pool, P.dsem("dbg3"), [(dbg_x2, xT[:])], reads=xT_r, is_output=True)

        out_d = [P.dsem("yout%d" % i) for i in range(4)]
        prefetched = set()

        def x_load(ti_, tb_):
            s_, s_r = stg[tb_ % 2]
            r0 = ti_ * NT + tb_ * 128
            P.dma(pool, stg_d[tb_ % 2], [(s_[:, :], xp[r0:r0 + 128, :])], writes=[s_r])

        for ti in range(NTILES + 1):
            sample = (ti == NTILES)
            N = NS if sample else NT
            if not sample:
                banks = [nb() for _ in range(8)]
                for tb in range(4):
                    s_, s_r = stg[tb % 2]
                    if (ti, tb) not in prefetched:
                        x_load(ti, tb)
                    for dc in range(8):
                        bk, bk_r = banks[dc]
                        transpose(bk[:, tb * 128:(tb + 1) * 128], bk_r, s_[:, dc * 128:(dc + 1) * 128], [s_r], 128, inc=(dc == 7 or tb == 3))
                for dc in range(8):
                    bk, bk_r = banks[dc]
                    if dc % 2 == 0:
                        P.op(dve, lambda: V.tensor_copy(xT[:, dc, :], bk[:, :]), reads=[bk_r], writes=[xT_r[dc]])
                    else:
                        P.op(act, lambda: S.copy(out=xT[:, dc, :], in_=bk[:, :]), reads=[bk_r], writes=[xT_r[dc]])
            else:
                s_, s_r = stg[0]
                P.dma(pool, stg_d[0], [(s_[0:NS, :], xs)], writes=[s_r])
                bk, bk_r = nb()
                for dc in range(8):
                    transpose(bk[:, dc * NS:(dc + 1) * NS], bk_r, s_[0:NS, dc * 128:(dc + 1) * 128], [s_r], NS, inc=(dc == 7))
                P.op(dve, lambda: V.tensor_copy(xT[:, :, 0:NS], bk[:, 0:8 * NS].rearrange("p (a b) -> p a b", a=8)), reads=[bk_r], writes=xT_r)

            for l in range(2):
                if sample:
                    P.dma(pool, d_c2c, [(nas[l, :, 0:29, :], sa[l, :, 1:30, :]), (nbs[l, :, 0:14, :], sbb[l, :, 1:15, :]),
                                        (nds[l, :, 0:1, :], sd[l, :, 1:2, :])], is_output=True)
                    for (src, T_, T_r, K) in ((sa, stAT, stAT_r, 30), (sbb, stBT, stBT_r, 15), (sd, stDT, stDT_r, 2)):
                        flat = src[l].rearrange("s k c -> (s k) c")
                        nrows = NS * K
                        r = 0
                        gi = 0
                        while r < nrows:
                            n = min(120, nrows - r)
                            s_, s_r = stg[gi % 2]
                            P.dma(pool, stg_d[gi % 2], [(s_[0:n, 0:256], flat[r:r + n, :])], writes=[s_r])
                            bk, bk_r = nb()
                            for ch in range(2):
                                transpose(bk[:, ch * 128:ch * 128 + n], bk_r, s_[0:n, ch * 128:(ch + 1) * 128], [s_r], n, inc=(ch == 1))
                            P.op(dve, lambda: V.tensor_copy(T_[:, :, r:r + n], bk[:, 0:256].rearrange("p (a b) -> p a b", a=2)[:, :, 0:n]),
                                 reads=[bk_r], writes=[T_r])
                            r += n
                            gi += 1
                block(l, ti, sample, have_stats=(l == 1))

            if not sample:
                if ti + 1 < NTILES:
                    for tb in range(2):
                        x_load(ti + 1, tb)
                        prefetched.add((ti + 1, tb))
                for tb in range(4):
                    for half in range(2):
                        bk, bk_r = nb()
                        for q in range(4):
                            dc = 4 * half + q
                            transpose(bk[:, q * 128:(q + 1) * 128], bk_r, xT[:, dc, tb * 128:(tb + 1) * 128], [xT_r[dc]], 128, inc=(q == 3))
                        oc = 2 * tb + half
                        if half == 0:
                            P.op(dve, lambda: V.tensor_copy(big[:, oc, :], bk[:, :]), reads=[bk_r], writes=[big_r[oc]])
                        else:
                            P.op(act, lambda: S.copy(out=big[:, oc, :], in_=bk[:, :]), reads=[bk_r], writes=[big_r[oc]])
                    r0 = ti * NT + tb * 128
                    P.dma(pool, out_d[tb], [(yp[r0:r0 + 128, :], big[:, 2 * tb:2 * tb + 2, :].rearrange("p a b -> p (a b)"))],
                          reads=[big_r[2 * tb], big_r[2 * tb + 1]], is_output=True)
            else:
                o_, o_r = ost[0]
                for half in range(2):
                    bk, bk_r = nb()
                    for q in range(4):
                        dc = 4 * half + q
                        transpose(bk[0:NS, q * 128:(q + 1) * 128], bk_r, xT[:, dc, 0:NS], [xT_r[dc]], 128, inc=(q == 3))
                    P.op(dve, lambda: V.tensor_copy(o_[0:NS, 512 * half:512 * half + 512], bk[0:NS, :]), reads=[bk_r], writes=[o_r])
                P.dma(pool, ost_d[0], [(ys, o_[0:NS, :])], reads=[o_r], is_output=True)

        assert wpos[0] == len(wseq), (wpos[0], len(wseq))
        P.finish()
        build_program.stats = (P.n_inst, P.n_sem, {e.name: e.cnt for e in P.engs})
    return nc


def _consts():
    ident = np.eye(128, dtype=np.float32)
    tril = np.tril(np.ones((128, 128), dtype=np.float32))
    invw = np.zeros((128, 2), np.float32)
    invcnt = np.zeros((128, 2, 16), np.float32)
    for ch in range(2):
        for half in range(2):
            w = POOLW[2 * ch + half]
            invw[64 * half:64 * half + 64, ch] = 1.0 / w
            for n in range(16):
                invcnt[64 * half:64 * half + 64, ch, n] = 1.0 / min(w, n + 1)
    return ident, tril, invw, invcnt


def kernel(x_prompt, x_sample, c_prompt, c_sample, state_conv_a, state_pool_b, state_conv_d,
           w_ada, b_ada, g_pre_mix, g_post_mix, w_in, conv_a_w, conv_a_b, ln_a_g, ln_a_b,
           pool_w, pool_scale, ln_c_g, ln_c_b, sgu_w, sgu_b, conv_d_w, w_out,
           g_pre_ffn, g_post_ffn, w_gate, w_up, w_down):
    f = lambda a: np.ascontiguousarray(np.asarray(a, dtype=np.float32))
    x_prompt, x_sample, c_prompt, c_sample = f(x_prompt), f(x_sample), f(c_prompt), f(c_sample)
    state_conv_a, state_pool_b, state_conv_d = f(state_conv_a), f(state_pool_b), f(state_conv_d)
    shared = dict(w_ada=f(w_ada), b_ada=f(b_ada), g_pre_mix=f(g_pre_mix), g_post_mix=f(g_post_mix), w_in=f(w_in),
                  conv_a_w=f(conv_a_w), conv_a_b=f(conv_a_b), ln_a_g=f(ln_a_g), ln_a_b=f(ln_a_b), pool_w=f(pool_w),
                  pool_scale=f(pool_scale), ln_c_g=f(ln_c_g), ln_c_b=f(ln_c_b), sgu_w=f(sgu_w), sgu_b=f(sgu_b),
                  conv_d_w=f(conv_d_w), w_out=f(w_out), g_pre_ffn=f(g_pre_ffn), g_post_ffn=f(g_post_ffn),
                  w_gate=f(w_gate), w_up=f(w_up), w_down=f(w_down))
    ident, tril, invw, invcnt = _consts()
    shared.update(c_ident=ident, c_tril=tril, c_invw=invw, c_invcnt=invcnt)
    n = 8
    in_maps = []
    for i in range(n):
        sl = slice(NS * i, NS * i + NS)
        m = dict(shared)
        m["xp"] = x_prompt[i]
        m["xs"] = np.ascontiguousarray(x_sample[sl, 0, :])
        m["cc"] = np.ascontiguousarray(np.concatenate([c_prompt[i:i + 1], c_sample[sl]], axis=0))
        m["sa"] = np.ascontiguousarray(state_conv_a[:, sl])
        m["sbb"] = np.ascontiguousarray(state_pool_b[:, sl])
        m["sd"] = np.ascontiguousarray(state_conv_d[:, sl])
        in_maps.append(m)
    nc = build_program()
    res = run_bass_kernel_spmd(nc, in_maps, core_ids=list(range(n)))
    R = res.results
    y_prompt = np.stack([R[i]["yp"] for i in range(n)], axis=0)
    y_sample = np.concatenate([R[i]["ys"] for i in range(n)], axis=0)[:, None, :]
    na_p = np.stack([R[i]["nap"] for i in range(n)], axis=1)
    nb_p = np.stack([R[i]["nbp"] for i in range(n)], axis=1)
    nd_p = np.stack([R[i]["ndp"] for i in range(n)], axis=1)
    na_s = np.concatenate([R[i]["nas"] for i in range(n)], axis=1)
    nb_s = np.concatenate([R[i]["nbs"] for i in range(n)], axis=1)
    nd_s = np.concatenate([R[i]["nds"] for i in range(n)], axis=1)
    nv_s = np.concatenate([R[i]["nvs"] for i in range(n)], axis=1)[:, :, None, :]
    outs = (y_prompt, y_sample, na_p, nb_p, nd_p, na_s, nb_s, nd_s, nv_s)
    return tuple(np.ascontiguousarray(o, dtype=np.float32) for o in outs)
```
